# Optimizing a Trainium2 kernel written in Bass

```python
import math
import jax, jax.numpy as jnp
from jax import lax
import numpy as np

D_MODEL = 1024
BATCH = 16
SEQ = 2048
DEPTH = 1

CHUNK = 64
N_MEM = 256
EPS = 1e-6
ROPE_THETA = 10000.0
NEG = -1e30

A_HEADS = 4
A_DK = 64
A_DV = 2 * A_DK
A_QK_W = A_HEADS * 2 * A_DK
A_WIDTH = A_HEADS * A_DV
Q_BLOCK = 128

B_HEADS = 8
B_DH = 64
B_WIDTH = B_HEADS * B_DH
B_LEFT_CHUNKS = 8
B_MAX_REL = 256

X_HEADS = 4
X_DH = D_MODEL // X_HEADS

D_FF = -(-(8 * D_MODEL) // (3 * 256)) * 256

OFF_QA = 0
OFF_KA = OFF_QA + A_QK_W
OFF_VA = OFF_KA + A_QK_W
OFF_QB = OFF_VA + A_WIDTH
OFF_KB = OFF_QB + B_WIDTH
OFF_VB = OFF_KB + B_WIDTH
OFF_GA = OFF_VB + B_WIDTH
OFF_GB = OFF_GA + D_MODEL
IN_COLS = OFF_GB + D_MODEL

kernel_name = "hybrid_diffattn_chunkband_memxattn_swiglu"


def rmsnorm(x, g):
    xf = x.astype(jnp.float32)
    y = xf * lax.rsqrt(jnp.mean(xf * xf, axis=-1, keepdims=True) + EPS)
    return (y * g.astype(jnp.float32)).astype(x.dtype)


def rope_tables(seq, dim):
    inv = 1.0 / (ROPE_THETA ** (jnp.arange(0, dim, 2, dtype=jnp.float32) / dim))
    ang = jnp.arange(seq, dtype=jnp.float32)[:, None] * inv[None, :]
    return jnp.cos(ang), jnp.sin(ang)


def apply_rope(x, cos, sin):
    x1, x2 = jnp.split(x.astype(jnp.float32), 2, axis=-1)
    out = jnp.concatenate([x1 * cos - x2 * sin, x2 * cos + x1 * sin], axis=-1)
    return out.astype(x.dtype)


def diff_attention(q, k, v, lam, subln_g, lam_init):
    b, s = q.shape[0], q.shape[1]
    nqb = s // Q_BLOCK
    scale = A_DK ** -0.5
    k_chunk = jnp.arange(s) // CHUNK
    qb = q.reshape(b, nqb, Q_BLOCK, A_HEADS, 2, A_DK).transpose(1, 0, 2, 3, 4, 5)

    def one_block(args):
        qi, blk = args
        sc = jnp.einsum('bqhcd,bkhcd->bchqk', qi, k).astype(jnp.float32) * scale
        q_chunk = (blk * Q_BLOCK + jnp.arange(Q_BLOCK)) // CHUNK
        mask = k_chunk[None, :] <= q_chunk[:, None]
        sc = jnp.where(mask[None, None, None], sc, NEG)
        p = jax.nn.softmax(sc, axis=-1)
        attn = p[:, 0] - lam * p[:, 1]
        return jnp.einsum('bhqk,bkhd->bqhd', attn.astype(v.dtype), v)

    o = lax.map(one_block, (qb, jnp.arange(nqb)))
    o = o.transpose(1, 0, 2, 3, 4).reshape(b, s, A_HEADS, A_DV)
    o = rmsnorm(o, subln_g) * (1.0 - lam_init)
    return o.reshape(b, s, A_WIDTH)


def chunk_band_attention(q, k, v, rel_bias):
    b, s = q.shape[0], q.shape[1]
    nc = s // CHUNK
    pad = B_LEFT_CHUNKS * CHUNK
    band = pad + CHUNK
    scale = B_DH ** -0.5
    kp = jnp.pad(k, ((0, 0), (pad, 0), (0, 0), (0, 0)))
    vp = jnp.pad(v, ((0, 0), (pad, 0), (0, 0), (0, 0)))
    i = jnp.arange(CHUNK)
    j = jnp.arange(band)
    dist = i[:, None] + pad - j[None, :]
    idx = jnp.clip(dist, -B_MAX_REL, B_MAX_REL) + B_MAX_REL
    bias = rel_bias[:, idx].astype(jnp.float32)
    qc = q.reshape(b, nc, CHUNK, B_HEADS, B_DH).transpose(1, 0, 2, 3, 4)

    def one_chunk(args):
        qi, c = args
        start = c * CHUNK
        kb = lax.dynamic_slice_in_dim(kp, start, band, axis=1)
        vb = lax.dynamic_slice_in_dim(vp, start, band, axis=1)
        sc = jnp.einsum('bqhd,bkhd->bhqk', qi, kb).astype(jnp.float32) * scale + bias[None]
        valid = (start + j) >= pad
        sc = jnp.where(valid[None, None, None, :], sc, NEG)
        p = jax.nn.softmax(sc, axis=-1)
        return jnp.einsum('bhqk,bkhd->bqhd', p.astype(vb.dtype), vb)

    o = lax.map(one_chunk, (qc, jnp.arange(nc)))
    return o.transpose(1, 0, 2, 3, 4).reshape(b, s, B_WIDTH)


def memory_cross_attention(h, m, w_cq, w_ckv, w_co):
    b, s = h.shape[0], h.shape[1]
    nm = m.shape[1]
    q = (h @ w_cq).reshape(b, s, X_HEADS, X_DH)
    k, v = jnp.split(m @ w_ckv, 2, axis=-1)
    k = k.reshape(b, nm, X_HEADS, X_DH)
    v = v.reshape(b, nm, X_HEADS, X_DH)
    sc = jnp.einsum('bshd,bmhd->bhsm', q, k).astype(jnp.float32) * (X_DH ** -0.5)
    p = jax.nn.softmax(sc, axis=-1)
    o = jnp.einsum('bhsm,bmhd->bshd', p.astype(v.dtype), v).reshape(b, s, D_MODEL)
    return o @ w_co


def swiglu(h, w_gate_up, w_down):
    g, u = jnp.split(h @ w_gate_up, 2, axis=-1)
    return (jax.nn.silu(g) * u) @ w_down


def setup_inputs(seed: int = 0) -> dict:
    key = jax.random.key(seed)
    ks = jax.random.split(key, 24)
    nrm = lambda k, shape, sc: jax.random.normal(k, shape, jnp.float32) * sc
    gain = lambda k, shape: 1.0 + 0.05 * jax.random.normal(k, shape, jnp.float32)
    L = DEPTH
    return {
        "x": nrm(ks[0], (BATCH, SEQ, D_MODEL), 1.0),
        "mem": nrm(ks[1], (BATCH, N_MEM, D_MODEL), 1.0),
        "norm_mix_g": gain(ks[2], (L, D_MODEL)),
        "w_in": nrm(ks[3], (L, D_MODEL, IN_COLS), D_MODEL ** -0.5),
        "lam_q1": nrm(ks[4], (L, A_DK), 0.1),
        "lam_k1": nrm(ks[5], (L, A_DK), 0.1),
        "lam_q2": nrm(ks[6], (L, A_DK), 0.1),
        "lam_k2": nrm(ks[7], (L, A_DK), 0.1),
        "subln_g": gain(ks[8], (L, A_DV)),
        "rel_bias": nrm(ks[9], (L, B_HEADS, 2 * B_MAX_REL + 1), 0.1),
        "w_up_a": nrm(ks[10], (L, A_WIDTH, D_MODEL), A_WIDTH ** -0.5),
        "w_up_b": nrm(ks[11], (L, B_WIDTH, D_MODEL), B_WIDTH ** -0.5),
        "w_out": nrm(ks[12], (L, D_MODEL, D_MODEL), D_MODEL ** -0.5),
        "norm_cross_g": gain(ks[13], (L, D_MODEL)),
        "norm_mem_g": gain(ks[14], (L, D_MODEL)),
        "w_cq": nrm(ks[15], (L, D_MODEL, D_MODEL), D_MODEL ** -0.5),
        "w_ckv": nrm(ks[16], (L, D_MODEL, 2 * D_MODEL), D_MODEL ** -0.5),
        "w_co": nrm(ks[17], (L, D_MODEL, D_MODEL), D_MODEL ** -0.5),
        "norm_ffn_g": gain(ks[18], (L, D_MODEL)),
        "w_gate_up": nrm(ks[19], (L, D_MODEL, 2 * D_FF), D_MODEL ** -0.5),
        "w_down": nrm(ks[20], (L, D_FF, D_MODEL), D_FF ** -0.5),
        "norm_final_g": gain(ks[21], (D_MODEL,)),
    }


def reference(x, mem, norm_mix_g, w_in, lam_q1, lam_k1, lam_q2, lam_k2, subln_g,
              rel_bias, w_up_a, w_up_b, w_out, norm_cross_g, norm_mem_g, w_cq,
              w_ckv, w_co, norm_ffn_g, w_gate_up, w_down, norm_final_g):
    b, s = x.shape[0], x.shape[1]
    cos, sin = rope_tables(s, A_DK)
    cos_a = cos[None, :, None, None, :]
    sin_a = sin[None, :, None, None, :]
    for layer in range(DEPTH):
        lam_init = 0.8 - 0.6 * math.exp(-0.3 * layer)
        h = rmsnorm(x, norm_mix_g[layer])
        z = h @ w_in[layer]
        qa = z[..., OFF_QA:OFF_KA].reshape(b, s, A_HEADS, 2, A_DK)
        ka = z[..., OFF_KA:OFF_VA].reshape(b, s, A_HEADS, 2, A_DK)
        va = z[..., OFF_VA:OFF_QB].reshape(b, s, A_HEADS, A_DV)
        qb = z[..., OFF_QB:OFF_KB].reshape(b, s, B_HEADS, B_DH)
        kb = z[..., OFF_KB:OFF_VB].reshape(b, s, B_HEADS, B_DH)
        vb = z[..., OFF_VB:OFF_GA].reshape(b, s, B_HEADS, B_DH)
        ga = z[..., OFF_GA:OFF_GB]
        gb = z[..., OFF_GB:IN_COLS]
        qa = apply_rope(qa, cos_a, sin_a)
        ka = apply_rope(ka, cos_a, sin_a)
        lam = (jnp.exp(jnp.sum(lam_q1[layer].astype(jnp.float32) * lam_k1[layer].astype(jnp.float32)))
               - jnp.exp(jnp.sum(lam_q2[layer].astype(jnp.float32) * lam_k2[layer].astype(jnp.float32)))
               + lam_init)
        ya = diff_attention(qa, ka, va, lam, subln_g[layer], lam_init)
        yb = chunk_band_attention(qb, kb, vb, rel_bias[layer])
        merged = (jax.nn.sigmoid(ga) * (ya @ w_up_a[layer])
                  + jax.nn.sigmoid(gb) * (yb @ w_up_b[layer]))
        x = x + merged @ w_out[layer]
        hc = rmsnorm(x, norm_cross_g[layer])
        mn = rmsnorm(mem, norm_mem_g[layer])
        x = x + memory_cross_attention(hc, mn, w_cq[layer], w_ckv[layer], w_co[layer])
        hf = rmsnorm(x, norm_ffn_g[layer])
        x = x + swiglu(hf, w_gate_up[layer], w_down[layer])
    return rmsnorm(x, norm_final_g)
```

```python
import math
from contextlib import ExitStack

import numpy as np
import concourse.bass as bass
import concourse.mybir as mybir
from concourse.bass_utils import run_bass_kernel_spmd

F32 = mybir.dt.float32
BF16 = mybir.dt.bfloat16
AF = mybir.ActivationFunctionType
ALU = mybir.AluOpType

S = 2048
D = 1024
NT = 16
EPS = 1e-6
NSLOT = 5
LAM_INIT = 0.8 - 0.6 * math.exp(0.0)


class _Op:
    __slots__ = ("eng", "fn", "deps", "signal", "sig", "dma_key", "dma_val", "pos")


class Prog:
    ENGS = ("pe", "act", "dve", "pool", "sp")

    def __init__(self):
        self.streams = {e: [] for e in self.ENGS}
        self.res = {}
        self.dma_cnt = {}
        self.arena_last = {}
        self.fences = {}

    def fence(self, arena):
        cur = dict(self.fences.get(arena, {}))
        for k, o in self.arena_last.get(arena, {}).items():
            c = cur.get(k)
            if c is None or self._later(o, c):
                cur[k] = o
        self.fences[arena] = cur

    @staticmethod
    def _later(a, b):
        if a.dma_key is not None:
            return a.dma_val > b.dma_val
        return a.pos > b.pos

    def op(self, eng, fn, reads=(), writes=(), dma_key=None):
        o = _Op()
        o.eng = eng
        o.fn = fn
        o.signal = False
        o.sig = 0
        o.dma_key = dma_key
        o.dma_val = 0
        o.pos = len(self.streams[eng])
        if dma_key is not None:
            self.dma_cnt[dma_key] = self.dma_cnt.get(dma_key, 0) + 16
            o.dma_val = self.dma_cnt[dma_key]
        deps = {}

        def dep(p, raw):
            if p is None or p is o:
                return
            if p.dma_key is not None:
                k = ("d", p.dma_key)
                c = deps.get(k)
                if c is None or p.dma_val > c.dma_val:
                    deps[k] = p
                return
            if p.eng == eng and dma_key is None:
                if eng == "pe":
                    return
            c = deps.get(p.eng)
            if c is None or p.pos > c.pos:
                deps[p.eng] = p

        arenas = set()
        for r in reads:
            arenas.add(r[0])
            st = self.res.get(r)
            if st:
                dep(st[0], True)
        for w in writes:
            arenas.add(w[0])
            st = self.res.get(w)
            if st:
                dep(st[0], False)
                for rd in st[1].values():
                    dep(rd, False)
        for a in arenas:
            for p in self.fences.get(a, {}).values():
                dep(p, True)
        me = eng if dma_key is None else ("d", dma_key)
        for r in reads:
            st = self.res.setdefault(r, [None, {}])
            st[1][me] = o
        for w in writes:
            self.res[w] = [o, {}]
        for a in arenas:
            self.arena_last.setdefault(a, {})[me] = o
        o.deps = list(deps.values())
        for p in o.deps:
            if p.dma_key is None:
                p.signal = True
        self.streams[eng].append(o)
        return o

    def emit(self, nc, es, final_waits):
        esem = {e: es.enter_context(nc.semaphore("sem_" + e)) for e in self.ENGS}
        dsem = {}
        for i, k in enumerate(self.dma_cnt):
            dsem[k] = es.enter_context(nc.semaphore("dsem%d" % i))
        for e in self.ENGS:
            c = 0
            for o in self.streams[e]:
                if o.signal and o.dma_key is None:
                    c += 1
                    o.sig = c
        block = es.enter_context(nc.Block())

        def run(engname, eng):
            waited = {}

            def wait(p):
                if p.dma_key is not None:
                    k = ("d", p.dma_key)
                    s = dsem[p.dma_key]
                    v = p.dma_val
                else:
                    k = p.eng
                    s = esem[p.eng]
                    v = p.sig
                if waited.get(k, 0) >= v:
                    return
                waited[k] = v
                eng.wait_ge(s, v)

            for o in self.streams[engname]:
                for p in o.deps:
                    wait(p)
                inst = o.fn(eng)
                if o.dma_key is not None:
                    inst.then_inc(dsem[o.dma_key], 16)
                elif o.signal:
                    inst.then_inc(esem[engname], 1)
            if engname == "sp":
                last = {}
                for p in final_waits:
                    c = last.get(p.dma_key)
                    if c is None or p.dma_val > c.dma_val:
                        last[p.dma_key] = p
                for p in last.values():
                    wait(p)

        @block.tensor
        def _(e):
            run("pe", e)

        @block.scalar
        def _(e):
            run("act", e)

        @block.vector
        def _(e):
            run("dve", e)

        @block.gpsimd
        def _(e):
            run("pool", e)

        @block.sync
        def _(e):
            run("sp", e)


def build(nseq=2, stage=99, dump=None):
    nc = bass.Bass("TRN2", target_bir_lowering=False)
    P = Prog()
    es = ExitStack()

    def din(name, shape, dt=F32):
        return nc.dram_tensor(name, list(shape), dt, kind="ExternalInput").ap()

    x_d = din("x", [nseq, S, D])
    mem_d = din("mem", [nseq, 256, D])
    W1 = din("W1", [1024, 6144])
    Wup = din("Wup", [512, 2048])
    wout_d = din("w_out", [1024, 1024])
    wcq_d = din("w_cq", [1024, 1024])
    wckv_d = din("w_ckv", [1024, 2048])
    wco_d = din("w_co", [1024, 1024])
    Wgu = din("Wgu", [1024, 5632])
    wdown_d = din("w_down", [2816, 1024])
    params_d = din("params", [128, 272])
    gains_d = din("gains", [5, 1024])
    rope_d = din("rope", [128, 4096])
    biasB_d = din("biasB", [128, 4096])
    ident_d = din("ident", [128, 128])
    perm_d = din("perm", [128, 128])
    sublng_d = din("sublng", [1, 128])
    out_d = nc.dram_tensor("out", [nseq, S, D], F32, kind="ExternalOutput").ap()
    dump_d = None
    if dump is not None:
        dump_d = nc.dram_tensor("dump", list(dump[1]), dump[2], kind="ExternalOutput").ap()

    def sb(name, shape, dt):
        return es.enter_context(nc.sbuf_tensor(name, list(shape), dt))

    R0 = sb("R0", [128, 32768], BF16)
    HT = sb("HT", [128, 16384], BF16)
    Y = sb("Y", [128, 16384], BF16)
    Z = sb("Z", [128, 8192], BF16)
    WS = sb("WS", [128, NSLOT * 4096], BF16)
    IDB = sb("IDB", [128, 128], BF16)
    PERM = sb("PERM", [128, 128], BF16)
    ONESB = sb("ONESB", [128, 128], BF16)
    PRM = sb("PRM", [128, 272], F32)
    GROW = sb("GROW", [128, 1024], F32)
    GSROW = sb("GSROW", [128, 128], F32)
    XS = sb("XS", [128, 2048], F32)
    HN = sb("HN", [128, 2048], BF16)
    ST = sb("ST", [128, 64], F32)
    GROW2 = sb("GROW2", [128, 1024], F32)
    LTMP = sb("LTMP", [128, 128], F32)
    PS = [es.enter_context(nc.psum_tensor("ps%d" % i, [128, 512], F32)) for i in range(8)]

    R0f = R0[:].bitcast(F32)
    Zf = Z[:].bitcast(F32)
    Yf = Y[:].bitcast(F32)
    hT = HT[:].rearrange("p (k t) -> p k t", t=S)
    yaT = Y[:, 0:8192].rearrange("p (k t) -> p k t", t=S)
    ybT = Y[:, 8192:16384].rearrange("p (k t) -> p k t", t=S)
    WSv = [WS[:, s * 4096:(s + 1) * 4096].rearrange("p (k c) -> p k c", c=512) for s in range(NSLOT)]
    Xv = R0f.rearrange("p (t c) -> p t c", c=1024)
    cosT = Zf[:, 0:2048]
    sinT = Zf[:, 2048:4096]

    def r1(M):
        return M.rearrange("(k p) c -> p k c", p=128)

    W1r, Wupr, woutr, wcqr, wckvr, wcor, Wgur, wdownr = (r1(M) for M in (W1, Wup, wout_d, wcq_d, wckv_d, wco_d, Wgu, wdown_d))

    class Rot:
        def __init__(self, ids):
            self.ids = ids
            self.i = 0

        def next(self):
            b = self.ids[self.i % len(self.ids)]
            self.i += 1
            return b

    mmpool = Rot([0, 1, 2])
    scpool = Rot([4, 5])
    accpool = Rot([6, 7])

    def psr(b):
        return ("PS", b)

    loads = []

    def ld_block(Mr, c0, kind):
        def f(s):
            return [(WSv[s][:, :, :], Mr[:, :, c0:c0 + 512])]
        loads.append((kind, f))

    def ld_merge(m):
        def f(s):
            return [(WSv[s][:, :, 0:256], W1r[:, :, 4096 + m * 256:4096 + (m + 1) * 256]),
                    (WSv[s][:, 0:4, 256:512], Wupr[:, :, m * 256:(m + 1) * 256])]
        loads.append(("merge", f))

    def ld_down(nb, g):
        nk = min(8, 22 - 8 * g)

        def f(s):
            return [(WSv[s][:, 0:nk, :], wdownr[:, 8 * g:8 * g + nk, nb * 512:(nb + 1) * 512])]
        loads.append(("down", f))

    for _b in range(nseq):
        ld_block(W1r, 2048, "vA")
        for h in range(4):
            ld_block(W1r, h * 512, "A")
        if stage >= 3:
            ld_block(W1r, 3584, "vB")
            for i in range(2):
                ld_block(W1r, 2560 + i * 512, "Bqk")
        if stage >= 4:
            for m in range(8):
                ld_merge(m)
            for i in range(4):
                ld_block(wckvr, i * 512, "ckv")
            for nb in range(2):
                ld_block(woutr, nb * 512, "out")
        if stage >= 5:
            for i in range(2):
                ld_block(wcqr, i * 512, "cq")
            for i in range(2):
                ld_block(wcor, i * 512, "co")
        if stage >= 6:
            for half in range(2):
                for j in range(11):
                    ld_block(Wgur, j * 512, "gu")
                for nb in range(2):
                    for g in range(3):
                        ld_down(nb, g)

    class WStream:
        def __init__(self):
            self.emitted = 0
            self.released = 0
            self.taken = 0

        def pump(self, limit=NSLOT):
            while self.emitted < len(loads) and self.emitted < self.released + limit:
                i = self.emitted
                s = i % NSLOT
                for (dst, src) in loads[i][1](s):
                    P.op("pool", lambda e, dst=dst, src=src: e.dma_start(out=dst, in_=src),
                         writes=[("WS", s)], dma_key=("ws", s))
                self.emitted += 1

        def take(self, kind):
            i = self.taken
            assert loads[i][0] == kind, (loads[i][0], kind)
            assert i < self.emitted, "weight slot ring too small at load %d (%s)" % (i, kind)
            self.taken += 1
            return i, i % NSLOT

        def release(self, i):
            assert i == self.released, (i, self.released)
            self.released += 1
            self.pump()

    ws = WStream()

    def mm(out, lhsT, rhs, start, stop, reads, writes, skip=False):
        P.op("pe", lambda e: e.matmul(out, lhsT=lhsT, rhs=rhs, start=start, stop=stop, skip_group_check=skip),
             reads=reads, writes=writes)

    def tr(out, in_, reads, writes):
        P.op("pe", lambda e: e.transpose(out, in_, IDB[:]), reads=list(reads) + [("M", "idb")], writes=writes)

    def act(out, in_, func, reads, writes, scale=None, bias=None, accum=None):
        kw = {}
        if scale is not None:
            kw["scale"] = scale
        if bias is not None:
            kw["bias"] = bias
        if accum is not None:
            kw["accum_out"] = accum
        P.op("act", lambda e: e.activation(out=out, in_=in_, func=func, **kw), reads=reads, writes=writes)

    def vcopy(eng, out, in_, reads, writes):
        if eng == "act":
            P.op("act", lambda e: e.copy(out=out, in_=in_), reads=reads, writes=writes)
        else:
            P.op(eng, lambda e: e.tensor_copy(out=out, in_=in_), reads=reads, writes=writes)

    def tt(eng, out, in0, in1, op, reads, writes):
        P.op(eng, lambda e: e.tensor_tensor(out=out, in0=in0, in1=in1, op=op), reads=reads, writes=writes)

    def tsc(out, in0, s1, s2, op0, op1, reads, writes, eng="dve"):
        if op1 is None:
            P.op(eng, lambda e: e.tensor_scalar(out=out, in0=in0, scalar1=s1, scalar2=None, op0=op0),
                 reads=reads, writes=writes)
        else:
            P.op(eng, lambda e: e.tensor_scalar(out=out, in0=in0, scalar1=s1, scalar2=s2, op0=op0, op1=op1),
                 reads=reads, writes=writes)

    def stt(out, in0, scalar, in1, op0, op1, reads, writes):
        P.op("dve", lambda e: e.scalar_tensor_tensor(out=out, in0=in0, scalar=scalar, in1=in1, op0=op0, op1=op1),
             reads=reads, writes=writes)

    def recip(out, in_, reads, writes):
        P.op("dve", lambda e: e.reciprocal(out=out, in_=in_), reads=reads, writes=writes)

    def dma(eng, out, in_, key, reads=(), writes=()):
        return P.op(eng, lambda e: e.dma_start(out=out, in_=in_), reads=reads, writes=writes, dma_key=key)

    def memset(eng, ap, val, writes):
        P.op(eng, lambda e: e.memset(ap, val), writes=writes)

    stcol = [0]

    def newstat(n=1):
        c = stcol[0]
        if c + n > 64:
            c = 0
        stcol[0] = c + n
        return ST[:, c:c + n], [("M", "st", i) for i in range(c, c + n)]

    dma("sp", PRM[:], params_d, ("c", 0), writes=[("M", "prm")])
    dma("pool", IDB[:], ident_d, ("c", 1), writes=[("M", "idb")])
    dma("pool", PERM[:], perm_d, ("c", 3), writes=[("M", "perm")])
    dma("sp", GSROW[:], sublng_d[0:1, :].partition_broadcast(128), ("c", 2), writes=[("M", "gsrow")])
    memset("dve", ONESB[:], 1.0, writes=[("M", "ones")])
    tsc(GSROW[:], GSROW[:], 1.0 - LAM_INIT, None, ALU.mult, None, reads=[("M", "gsrow")], writes=[("M", "gsrow")])
    lt, ltr = newstat(4)
    tt("dve", LTMP[:, 0:64], PRM[:, 0:64], PRM[:, 64:128], ALU.mult, reads=[("M", "prm")], writes=[("M", "junkd")])
    P.op("dve", lambda e: e.tensor_reduce(out=lt[:, 0:1], in_=LTMP[:, 0:64], axis=mybir.AxisListType.X, op=ALU.add),
         reads=[("M", "junkd")], writes=[ltr[0]])
    tt("dve", LTMP[:, 64:128], PRM[:, 128:192], PRM[:, 192:256], ALU.mult, reads=[("M", "prm")], writes=[("M", "junkd2")])
    P.op("dve", lambda e: e.tensor_reduce(out=lt[:, 1:2], in_=LTMP[:, 64:128], axis=mybir.AxisListType.X, op=ALU.add),
         reads=[("M", "junkd2")], writes=[ltr[1]])
    act(lt[:, 2:4], lt[:, 0:2], AF.Exp, reads=ltr[0:2], writes=ltr[2:4])
    NEGLAM = sb("NEGLAM", [128, 2], F32)
    tt("dve", NEGLAM[:, 0:1], lt[:, 3:4], lt[:, 2:3], ALU.subtract, reads=ltr[2:4], writes=[("M", "nl0")])
    tsc(NEGLAM[:, 1:2], NEGLAM[:, 0:1], -LAM_INIT, None, ALU.add, None, reads=[("M", "nl0")], writes=[("M", "neglam")])
    neglam = NEGLAM[:, 1:2]
    ws.pump(limit=2)

    out_dmas = []

    GR = [GROW, GROW2]

    def load_grow(gi, slot):
        dma("sp", GR[slot][:], gains_d[gi:gi + 1, :].partition_broadcast(128), ("grow", slot),
            writes=[("M", "grow", slot)])

    ctr = {"hn": 0, "ev": 0, "xs": 0}

    def rstd_stats(src, sres, junk, junk_res, n_elems):
        ss, ssr = newstat()
        ln, lnr = newstat()
        rs, rsr = newstat()
        act(junk, src, AF.Square, reads=sres, writes=list(junk_res) + ssr, accum=ss)
        act(ln, ss, AF.Ln, reads=ssr, writes=lnr, scale=1.0 / n_elems, bias=EPS)
        act(rs, ln, AF.Exp, reads=lnr, writes=rsr, scale=-0.5)
        return rs, rsr

    def norm_A(src, sres, gslot):
        i = ctr["hn"] % 2
        ctr["hn"] += 1
        hn = HN[:, i * 1024:(i + 1) * 1024]
        hnr = ("M", "hn", i)
        rs, rsr = rstd_stats(src, sres, hn, [hnr], 1024)
        stt(hn, src, rs, GR[gslot][:], ALU.mult, ALU.mult, reads=list(sres) + rsr + [("M", "grow", gslot)],
            writes=[hnr])
        return hn, hnr

    def norm_B(pair, dstT, t, dres):
        hn, hnr = pair
        b_ = mmpool.next()
        pb = PS[b_][:].bitcast(BF16)
        for k in range(8):
            tr(pb[:, k * 128:(k + 1) * 128], hn[:, k * 128:(k + 1) * 128], reads=[hnr], writes=[psr(b_)])
        e_ = "dve" if ctr["ev"] % 2 == 0 else "act"
        ctr["ev"] += 1
        vcopy(e_, dstT[:, :, t * 128:(t + 1) * 128], pb.rearrange("p (k c) -> p k c", c=128), reads=[psr(b_)],
              writes=[dres])

    def xload(src_d, b, t):
        i = ctr["xs"] % 2
        ctr["xs"] += 1
        xs = XS[:, i * 1024:(i + 1) * 1024]
        dma("sp", xs, src_d[b, t * 128:(t + 1) * 128, :], ("xs", i), writes=[("M", "xs", i)])
        return xs, [("M", "xs", i)]

    def p0_iter(b, after_tile=None):
        pend = None
        nxt = xload(x_d, b, 0)
        for t in range(NT):
            xs, xsr = nxt
            if t + 1 < NT:
                nxt = xload(x_d, b, t + 1)
            cur = (norm_A(xs, xsr, 1), t)
            if pend is not None:
                norm_B(pend[0], hT, pend[1], ("HT", pend[1]))
                if after_tile is not None:
                    after_tile(pend[1])
            pend = cur
            yield
        norm_B(pend[0], hT, pend[1], ("HT", pend[1]))
        if after_tile is not None:
            after_tile(pend[1])
        yield

    def xres(t):
        return Xv[:, t, :], [("R0", "X", t, 0), ("R0", "X", t, 1)]

    def do_dump(ap, reads):
        o = dma("sp", dump_d, ap, ("dump",), reads=reads)
        out_dmas.append(o)

    assert stage == 99
    load_grow(0, 1)
    vAaug0 = R0[:, 12288:12288 + 16 * 4 * 129].rearrange("p (t h d) -> p t h d", h=4, d=129)
    mmV = Rot([1, 2, 3])

    def vA_tile(t, s, vAaug_):
        bk = mmV.next()
        for k in range(8):
            mm(PS[bk][:, :], hT[:, k, t * 128:(t + 1) * 128], WSv[s][:, k, :], k == 0, k == 7,
               reads=[("HT", t), ("WS", s)], writes=[psr(bk)])
        vcopy("act" if t % 2 == 0 else "dve", vAaug_[:, t, :, 0:128],
              PS[bk][:].rearrange("p (h d) -> p h d", d=128), reads=[psr(bk)], writes=[("R0", "vA", t)])

    wiv0, sv0 = ws.take("vA")
    for n_, _ in enumerate(p0_iter(0, after_tile=lambda t: vA_tile(t, sv0, vAaug0))):
        if n_ == 8:
            ws.pump()
    ws.release(wiv0)
    ws.pump()
    for b in range(nseq):
        for a in ("R0", "Y", "Z"):
            P.fence(a)
        dma("sp", Zf, rope_d, ("rope",), writes=[("Z", "rope")])

        OFF_QK = 0
        OFF_VA = 12288
        OFF_PT = 20608
        OFF_PTD = 22656
        OFF_ON0 = 26752
        OFF_OT = 27776
        OFF_YTOK = 28800
        qk = [R0[:, OFF_QK + i * 6144:OFF_QK + (i + 1) * 6144].rearrange("p (w t) -> p w t", t=S) for i in range(2)]
        vAaug = R0[:, OFF_VA:OFF_VA + 16 * 4 * 129].rearrange("p (t h d) -> p t h d", h=4, d=129)
        pTs = [R0[:, OFF_PT + i * 512:OFF_PT + (i + 1) * 512] for i in range(4)]
        pTd = [R0[:, OFF_PTD + i * 512:OFF_PTD + (i + 1) * 512] for i in range(8)]
        On0 = R0f[:, OFF_ON0 // 2:OFF_ON0 // 2 + 512].rearrange("p (j d) -> p j d", d=128)
        Ot = R0f[:, OFF_OT // 2:OFF_OT // 2 + 512].rearrange("p (j d) -> p j d", d=128)
        ytoks = [R0[:, OFF_YTOK + i * 512:OFF_YTOK + (i + 1) * 512].rearrange("p (j d) -> p j d", d=128) for i in range(2)]
        accsetsA = Rot([(6, 7), (2, 3)])
        mmA4 = Rot([0, 1, 2, 3])
        trA = Rot([0])
        scA = Rot([4, 5, 1])
        pend_tr = []
        ytog = [0]
        zb = [R0[:, 29824 + i * 512:29824 + (i + 1) * 512] for i in range(2)]
        zbc = [0]
        rt = [Yf[:, 4096 + i * 512:4096 + (i + 1) * 512] for i in range(4)]
        for i in range(2):
            memset("pool", qk[i][64:128, 1, :], 0.0, writes=[("R0", "kzz", i, 0)])
            memset("pool", qk[i][0:64, 2, :], 0.0, writes=[("R0", "kzz", i, 1)])

        memset("pool", vAaug[:, :, :, 128:129], 1.0, writes=[("R0", "vAones")])

        if b > 0:
            wi, s = ws.take("vA")
            for t in range(NT):
                vA_tile(t, s, vAaug)
            ws.release(wi)

        pti = [0]
        for h in range(4):
            wi, s = ws.take("A")
            qkb = qk[h % 2]
            pendR = None

            def rope_finish(item, h=h, qkb=qkb):
                T, w, bz, zi = item
                br = mmA4.next()
                mm(PS[br][:, :], PERM[:, :], zb[zi], True, True, reads=[("R0", "zb", zi), ("M", "perm")],
                   writes=[psr(br)])
                ia = 2 * (w % 2)
                tt("dve", rt[ia], PS[bz][:, :], cosT[:, T * 512:(T + 1) * 512], ALU.mult,
                   reads=[psr(bz), ("Z", "rope"), ("R0", "zb", zi)], writes=[("Y", "rt", ia)])
                tt("dve", rt[ia + 1], PS[br][:, :], sinT[:, T * 512:(T + 1) * 512], ALU.mult,
                   reads=[psr(br), ("Z", "rope")], writes=[("Y", "rt", ia + 1)])
                if w == 0:
                    tt("pool", qkb[:, 0, T * 512:(T + 1) * 512], rt[ia], rt[ia + 1], ALU.add,
                       reads=[("Y", "rt", ia), ("Y", "rt", ia + 1)], writes=[("R0", "qk", h % 2, 0, T)])
                else:
                    for c in range(2):
                        prc = slice(c * 64, c * 64 + 64)
                        tt("pool", qkb[prc, 1 + c, T * 512:(T + 1) * 512], rt[ia][prc, :], rt[ia + 1][prc, :], ALU.add,
                           reads=[("Y", "rt", ia), ("Y", "rt", ia + 1)], writes=[("R0", "qk", h % 2, 1 + c, T)])

            for T in range(4):
                for w in range(2):
                    bz = mmA4.next()
                    c0 = w * 256
                    for k in range(8):
                        mm(PS[bz][:, :], WSv[s][:, k, c0:c0 + 128], hT[:, k, T * 512:(T + 1) * 512], k == 0, k == 7,
                           reads=[("WS", s)] + [("HT", T * 4 + i) for i in range(4)], writes=[psr(bz)])
                    zi = zbc[0] % 2
                    zbc[0] += 1
                    vcopy("act", zb[zi], PS[bz][:, :], reads=[psr(bz)], writes=[("R0", "zb", zi)])
                    if pendR is not None:
                        rope_finish(pendR)
                    pendR = (T, w, bz, zi)
            rope_finish(pendR)
            ws.release(wi)
            if stage == 2 and dump is not None and dump[0] == "qk" and h == 0:
                do_dump(qkb[:, 0:2, :], [("R0", "qk", 0, w, T) for w in range(2) for T in range(4)])

            pendA = []

            def emit_pvA(item, h=h):
                g, i, buf, bres, r = item
                for jj in range(max(0, r), 4):
                    bkx = jj // 2
                    st_ = not g["started"][bkx]
                    g["started"][bkx] = True
                    accv = PS[g["accb"][bkx]][:, (jj % 2) * 256:(jj % 2) * 256 + 129]
                    mm(accv, buf[:, jj * 128:(jj + 1) * 128], vAaug[:, i, h, :], st_, i == g["last"],
                       reads=list(bres) + [("R0", "vA", i), ("R0", "vAones")], writes=[psr(g["accb"][bkx])], skip=True)
                if i == g["last"]:
                    finish_group(g)

            def finish_group(g, h=h):
                accb = g["accb"]
                Q = g["Q"]
                c = g["c"]

                def accv(jj):
                    return PS[accb[jj // 2]][:, (jj % 2) * 256:(jj % 2) * 256 + 129]

                rd, rdr = newstat(4)
                for bkx in range(2):
                    recip(rd[:, 2 * bkx:2 * bkx + 2], PS[accb[bkx]][:, 128:512:256], reads=[psr(accb[bkx])],
                          writes=rdr[2 * bkx:2 * bkx + 2])
                if c == 0:
                    for jj in range(4):
                        tsc(On0[:, jj, :], accv(jj)[:, 0:128], rd[:, jj:jj + 1], None, ALU.mult, None,
                            reads=[psr(accb[jj // 2]), rdr[jj]], writes=[("R0", "On0", jj)])
                    return
                rl, rlr = newstat(4)
                tsc(rl, rd, neglam, None, ALU.mult, None, reads=rdr + [("M", "neglam")], writes=rlr)
                ssq, ssr = newstat(4)
                yi = ytog[0] % 2
                ytog[0] += 1
                ytk = ytoks[yi]
                for jj in range(4):
                    stt(Ot[:, jj, :], accv(jj)[:, 0:128], rl[:, jj:jj + 1], On0[:, jj, :], ALU.mult, ALU.add,
                        reads=[psr(accb[jj // 2]), rlr[jj], ("R0", "On0", jj)], writes=[("R0", "Ot", jj)])
                    act(ytk[:, jj, :], Ot[:, jj, :], AF.Square, reads=[("R0", "Ot", jj)],
                        writes=[ssr[jj], ("R0", "ytok", yi, jj)], accum=ssq[:, jj:jj + 1])
                ln, lnr = newstat(4)
                rs, rsr = newstat(4)
                act(ln, ssq, AF.Ln, reads=ssr, writes=lnr, scale=1.0 / 128, bias=EPS)
                act(rs, ln, AF.Exp, reads=lnr, writes=rsr, scale=-0.5)
                for jj in range(4):
                    stt(ytk[:, jj, :], Ot[:, jj, :], rs[:, jj:jj + 1], GSROW[:], ALU.mult, ALU.mult,
                        reads=[("R0", "Ot", jj), rsr[jj], ("M", "gsrow")], writes=[("R0", "ytok", yi, jj)])

                def fin(h=h, Q=Q, ytk=ytk, yi=yi):
                    bt = trA.next()
                    pb = PS[bt][:].bitcast(BF16)
                    for jj in range(4):
                        tr(pb[:, jj * 128:(jj + 1) * 128], ytk[:, jj, :], reads=[("R0", "ytok", yi, jj)],
                           writes=[psr(bt)])
                    vcopy("dve", yaT[:, h, Q * 512:(Q + 1) * 512], pb[:, 0:512], reads=[psr(bt)],
                          writes=[("Y", "yaT", h, Q)])

                pend_tr.append([fin, 6])

            for Q in range(4):
                nkt = 4 * Q + 4
                for c in range(2):
                    diag = [4 * Q + 3, 4 * Q + 2, 4 * Q + 1, 4 * Q]
                    full = list(range(4 * Q))
                    order = []
                    while diag or full:
                        if diag:
                            order.append(diag.pop(0))
                        if full:
                            order.append(full.pop(0))
                    g = {"accb": list(accsetsA.next()), "started": [False, False], "nkt": nkt, "Q": Q, "c": c,
                         "last": order[-1]}
                    for i in order:
                        r = i - 4 * Q
                        c0 = max(0, r) * 128
                        sb_ = scA.next()
                        qres = [("R0", "qk", h % 2, 0, Q), ("R0", "qk", h % 2, 1 + c, i // 4), ("R0", "kzz", h % 2, c)]
                        mm(PS[sb_][:, c0:512], qkb[:, 1 + c, i * 128:(i + 1) * 128], qkb[:, 0, Q * 512 + c0:(Q + 1) * 512],
                           True, True, reads=qres, writes=[psr(sb_)])
                        if r < 0:
                            bi = pti[0] % 4
                            pti[0] += 1
                            buf = pTs[bi]
                            bres = [("R0", "pT", bi)]
                            act(buf[:, :], PS[sb_][:, :], AF.Exp, reads=[psr(sb_)], writes=bres, scale=0.125)
                        else:
                            bi = r + 4 * (pti[0] % 2)
                            pti[0] += 1
                            buf = pTd[bi]
                            bres = [("R0", "pTdd", bi), ("R0", "pTdd2", bi)]
                            act(buf[:, c0:c0 + 64], PS[sb_][:, c0:c0 + 64], AF.Exp, reads=[psr(sb_), ("M", "prm")],
                                writes=[bres[0]], scale=0.125, bias=PRM[:, 265:266])
                            act(buf[:, c0 + 64:512], PS[sb_][:, c0 + 64:512], AF.Exp, reads=[psr(sb_)],
                                writes=[bres[1]], scale=0.125)
                        pendA.append((g, i, buf, bres, r))
                        if len(pendA) > 2:
                            emit_pvA(pendA.pop(0))
                        for ent in list(pend_tr):
                            ent[1] -= 1
                            if ent[1] <= 0:
                                pend_tr.remove(ent)
                                ent[0]()
            while pendA:
                emit_pvA(pendA.pop(0))
        while pend_tr:
            pend_tr.pop(0)[0]()

        P.fence("R0")
        P.fence("Y")
        OFFB_V = 8192
        OFFB_BIAS = 16512
        OFFB_TMP = 24704
        OFFB_PT = 26752
        OFFB_YTOK = 29312
        qkB = [R0[:, i * 4096:(i + 1) * 4096].rearrange("p (w t) -> p w t", t=S) for i in range(2)]
        vBaug = R0[:, OFFB_V:OFFB_V + 16 * 8 * 65].rearrange("p (t h d) -> p t h d", h=8, d=65)
        biasT = R0f[:, OFFB_BIAS // 2:OFFB_BIAS // 2 + 4096].rearrange("p (h c) -> p h c", c=512)
        tmpS = [R0f[:, OFFB_TMP // 2 + i * 512:OFFB_TMP // 2 + (i + 1) * 512] for i in range(2)]
        pTB = [R0[:, OFFB_PT + i * 640:OFFB_PT + (i + 1) * 640] for i in range(4)]
        ybtok = R0[:, OFFB_YTOK:OFFB_YTOK + 2048].rearrange("p (j d) -> p j d", d=128)

        dma("sp", R0f[:, OFFB_BIAS // 2:OFFB_BIAS // 2 + 4096], biasB_d, ("biasB",), writes=[("R0", "biasT")])
        memset("pool", vBaug[:, :, :, 64:65], 1.0, writes=[("R0", "vBones")])

        mmB = Rot([0, 1])
        scpairs = Rot([(4, 2), (5, 3)])
        wiv, sv = ws.take("vB")
        for t in range(NT):
            bk = mmB.next()
            for k in range(8):
                mm(PS[bk][:, :], hT[:, k, t * 128:(t + 1) * 128], WSv[sv][:, k, :], k == 0, k == 7,
                   reads=[("HT", t), ("WS", sv)], writes=[psr(bk)])
            vcopy("act" if t % 2 == 0 else "dve", vBaug[:, t, :, 0:64],
                  PS[bk][:].rearrange("p (h d) -> p h d", d=64), reads=[psr(bk)], writes=[("R0", "vB", t)])
        ws.release(wiv)
        wq = [ws.take("Bqk"), ws.take("Bqk")]
        cnt = {"tmp": 0, "pt": 0}
        for m in range(4):
            wi, s = wq[m // 2]
            cb = (m % 2) * 256
            qkm = qkB[m % 2]
            for T in range(4):
                for w in range(2):
                    bk = mmB.next()
                    for k in range(8):
                        mm(PS[bk][:, :], WSv[s][:, k, cb + w * 128:cb + (w + 1) * 128], hT[:, k, T * 512:(T + 1) * 512],
                           k == 0, k == 7, reads=[("WS", s)] + [("HT", T * 4 + i) for i in range(4)], writes=[psr(bk)])
                    vcopy("act" if w == 0 else "dve", qkm[:, w, T * 512:(T + 1) * 512], PS[bk][:, :],
                          reads=[psr(bk)], writes=[("R0", "qkB", m % 2, w, T)])
            if m % 2 == 1:
                ws.release(wi)
            chains = []
            for hh in range(2):
                chains.append({"hh": hh, "h": 2 * m + hh, "pr": slice(hh * 64, hh * 64 + 64), "accb": 6 + hh,
                               "sc": [(4, 2), (5, 3)][hh], "pend": None, "k": 0})

            def emit_pvB(ch, item):
                j, tiles, ptb, ptres = item
                jj = j % 4
                accb = ch["accb"]
                h = ch["h"]
                accv = PS[accb][:, jj * 128:jj * 128 + 65]
                for n, (t, col) in enumerate(tiles):
                    mm(accv, ptb[:, col:col + 128], vBaug[:, j - t, h, :], n == 0, n == len(tiles) - 1,
                       reads=list(ptres) + [("R0", "vB", j - t), ("R0", "vBones")], writes=[psr(accb)], skip=True)

            def normB(ch, J):
                accb = ch["accb"]
                hh = ch["hh"]
                rd, rdr = newstat(4)
                recip(rd, PS[accb][:, 64:512:128], reads=[psr(accb)], writes=rdr)
                for jj in range(4):
                    j = 4 * J + jj
                    tsc(ybtok[:, j, hh * 64:(hh + 1) * 64], PS[accb][:, jj * 128:jj * 128 + 64], rd[:, jj:jj + 1], None,
                        ALU.mult, None, reads=[psr(accb), rdr[jj]], writes=[("R0", "ybtok", j, hh)])

            def stepB(ch, j):
                h = ch["h"]
                hh = ch["hh"]
                pr = ch["pr"]
                cbias = PRM[:, 257 + h:258 + h]
                sb_, sb3 = ch["sc"]
                tiles = []
                for t in range(min(3, j + 1)):
                    tiles.append((t, t * 128))
                nd = len(tiles)
                if j >= 4:
                    tiles.append((4, 384))
                    nd = 4
                for (t, col) in tiles:
                    mm(PS[sb_][:, col:col + 128], qkm[pr, 1, (j - t) * 128:(j - t + 1) * 128],
                       qkm[pr, 0, j * 128:(j + 1) * 128], True, True,
                       reads=[("R0", "qkB", m % 2, 1, (j - t) // 4), ("R0", "qkB", m % 2, 0, j // 4)],
                       writes=[psr(sb_)], skip=True)
                if j >= 3:
                    mm(PS[sb3][:, 0:128], qkm[pr, 1, (j - 3) * 128:(j - 2) * 128],
                       qkm[pr, 0, j * 128:(j + 1) * 128], True, True,
                       reads=[("R0", "qkB", m % 2, 1, (j - 3) // 4), ("R0", "qkB", m % 2, 0, j // 4)],
                       writes=[psr(sb3)], skip=True)
                ti = hh
                pi = 2 * hh + ch["k"] % 2
                ch["k"] += 1
                ptb = pTB[pi]
                ptres = [("R0", "pTB", pi, 0)]
                stt(tmpS[ti][:, 0:nd * 128], PS[sb_][:, 0:nd * 128], 0.125, biasT[:, h, 0:nd * 128], ALU.mult, ALU.add,
                    reads=[psr(sb_), ("R0", "biasT")], writes=[("R0", "tmpS", ti)])
                act(ptb[:, 0:nd * 128], tmpS[ti][:, 0:nd * 128], AF.Exp, reads=[("R0", "tmpS", ti)],
                    writes=[("R0", "pTB", pi, 0)])
                if j >= 3:
                    act(ptb[:, 512:640], PS[sb3][:, 0:128], AF.Exp, reads=[psr(sb3), ("M", "prm")],
                        writes=[("R0", "pTB", pi, 1)], scale=0.125, bias=cbias)
                    ptres.append(("R0", "pTB", pi, 1))
                    tiles = tiles + [(3, 512)]
                if ch["pend"] is not None:
                    pj = ch["pend"][0]
                    emit_pvB(ch, ch["pend"])
                    if pj % 4 == 3:
                        normB(ch, pj // 4)
                ch["pend"] = (j, tiles, ptb, ptres)

            for j in range(NT):
                for ch in chains:
                    stepB(ch, j)
            for ch in chains:
                emit_pvB(ch, ch["pend"])
                normB(ch, 3)
            for J in range(4):
                bt = mmB.next()
                pb = PS[bt][:].bitcast(BF16)
                for jj in range(4):
                    j = 4 * J + jj
                    tr(pb[:, jj * 128:(jj + 1) * 128], ybtok[:, j, :], reads=[("R0", "ybtok", j, 0), ("R0", "ybtok", j, 1)],
                       writes=[psr(bt)])
                vcopy("act", ybT[:, m, J * 512:(J + 1) * 512], pb[:, 0:512], reads=[psr(bt)], writes=[("Y", "ybT", m, J)])

        P.fence("R0")
        P.fence("Z")
        mnT = Z[:, 0:2048].rearrange("p (k t) -> p k t", t=256)
        KT = Z[:, 2048:4096].rearrange("p (k t) -> p k t", t=256)
        Vx = Z[:, 4096:6144].rearrange("p (mt c) -> p mt c", c=1024)
        rdn = Zf[:, 3072:3584]
        lnd = Zf[:, 3584:4096]
        load_grow(2, 1)
        for t in range(2):
            xs, xsr = xload(mem_d, b, t)
            norm_B(norm_A(xs, xsr, 1), mnT, t, ("Z", "mnT", t))
        if b + 1 < nseq:
            load_grow(0, 1)
        load_grow(1, 0)
        bigpool = Rot([0, 1, 2, 4, 5, 6, 7])
        mtmp = [R0f[:, i * 512:(i + 1) * 512] for i in range(8)]
        mT = R0[:, 16384:32768].rearrange("p (t m c) -> p t m c", m=8, c=128)
        mcount = 0
        for m in range(8):
            wi, s = ws.take("merge")
            for T in range(4):
                par = mcount % 2
                mcount += 1
                sa, sbb, t1, t2 = (mtmp[par * 4 + i] for i in range(4))
                rsa, rsb, rt1, rt2 = (("R0", "mtmp", par * 4 + i) for i in range(4))
                hres = [("HT", T * 4 + i) for i in range(4)]
                bA = bigpool.next()
                for k in range(8):
                    mm(PS[bA][:, :], WSv[s][:, k, 0:128], hT[:, k, T * 512:(T + 1) * 512], k == 0, k == 7,
                       reads=[("WS", s)] + hres, writes=[psr(bA)])
                act(sa, PS[bA][:, :], AF.Sigmoid, reads=[psr(bA)], writes=[rsa])
                bB = bigpool.next()
                for k in range(8):
                    mm(PS[bB][:, :], WSv[s][:, k, 128:256], hT[:, k, T * 512:(T + 1) * 512], k == 0, k == 7,
                       reads=[("WS", s)] + hres, writes=[psr(bB)])
                act(sbb, PS[bB][:, :], AF.Sigmoid, reads=[psr(bB)], writes=[rsb])
                bUa = bigpool.next()
                for k in range(4):
                    mm(PS[bUa][:, :], WSv[s][:, k, 256:384], yaT[:, k, T * 512:(T + 1) * 512], k == 0, k == 3,
                       reads=[("WS", s), ("Y", "yaT", k, T)], writes=[psr(bUa)])
                tt("dve", t1, sa, PS[bUa][:, :], ALU.mult, reads=[rsa, psr(bUa)], writes=[rt1])
                bUb = bigpool.next()
                for k in range(4):
                    mm(PS[bUb][:, :], WSv[s][:, k, 384:512], ybT[:, k, T * 512:(T + 1) * 512], k == 0, k == 3,
                       reads=[("WS", s), ("Y", "ybT", k, T)], writes=[psr(bUb)])
                tt("dve", t2, sbb, PS[bUb][:, :], ALU.mult, reads=[rsb, psr(bUb)], writes=[rt2])
                tt("pool", mT[:, 4 * T:4 * T + 4, m, :], t1.rearrange("p (j c) -> p j c", c=128),
                   t2.rearrange("p (j c) -> p j c", c=128), ALU.add, reads=[rt1, rt2], writes=[("R0", "mT", m, T)])
            ws.release(wi)

        mres = [("Z", "mnT", 0), ("Z", "mnT", 1)]
        for i in range(2):
            wi, s = ws.take("ckv")
            for cc in range(4):
                c = i * 4 + cc
                bk = mmpool.next()
                for k in range(8):
                    mm(PS[bk][:, 0:256], WSv[s][:, k, cc * 128:(cc + 1) * 128], mnT[:, k, :], k == 0, k == 7,
                       reads=[("WS", s)] + mres, writes=[psr(bk)])
                vcopy("act" if cc % 2 == 0 else "dve", KT[:, c, :], PS[bk][:, 0:256], reads=[psr(bk)],
                      writes=[("Z", "KT", c)])
            ws.release(wi)
        for nb in range(2):
            wi, s = ws.take("ckv")
            for mt in range(2):
                bk = mmpool.next()
                for k in range(8):
                    mm(PS[bk][:, :], mnT[:, k, mt * 128:(mt + 1) * 128], WSv[s][:, k, :], k == 0, k == 7,
                       reads=[("WS", s)] + mres, writes=[psr(bk)])
                vcopy("act" if mt == 0 else "dve", Vx[:, mt, nb * 512:(nb + 1) * 512], PS[bk][:, :], reads=[psr(bk)],
                      writes=[("Z", "Vx", mt, nb)])
            ws.release(wi)

        P.fence("R0")
        wo = [ws.take("out"), ws.take("out")]
        pend = None
        for t in range(NT):
            xs, xsr = xload(x_d, b, t)
            for nb in range(2):
                bk = mmpool.next()
                s = wo[nb][1]
                for m in range(8):
                    mm(PS[bk][:, :], mT[:, t, m, :], WSv[s][:, m, :], m == 0, m == 7,
                       reads=[("WS", s), ("R0", "mT", m, t // 4)], writes=[psr(bk)])
                tt("dve", Xv[:, t, nb * 512:(nb + 1) * 512], xs[:, nb * 512:(nb + 1) * 512], PS[bk][:, :], ALU.add,
                   reads=xsr + [psr(bk)], writes=[("R0", "X", t, nb)])
            cur = (norm_A(*xres(t), 0), t)
            if pend is not None:
                norm_B(pend[0], hT, pend[1], ("HT", pend[1]))
            pend = cur
        norm_B(pend[0], hT, pend[1], ("HT", pend[1]))
        ws.release(wo[0][0])
        ws.release(wo[1][0])
        load_grow(3, 0)

        P.fence("Y")
        mmX = Rot([0, 1, 2, 3, 6, 7])
        qT = Y[:, 6144:10240].rearrange("p (k t) -> p k t", t=512)
        oT = Y[:, 10240:14336].rearrange("p (k t) -> p k t", t=512)
        pX = [Y[:, 14336 + i * 512:14336 + (i + 1) * 512] for i in range(4)]
        wq_ = [ws.take("cq"), ws.take("cq")]
        wo_ = [ws.take("co"), ws.take("co")]
        pxc = [0]
        pendF = None
        qTs = [qT, Y[:, 0:4096].rearrange("p (k t) -> p k t", t=512)]
        scX = Rot([4, 5])

        def qproj(T):
            hres = [("HT", T * 4 + i) for i in range(4)]
            qb_ = qTs[T % 2]
            for c in range(8):
                s = wq_[c // 4][1]
                bk = mmX.next()
                for k in range(8):
                    mm(PS[bk][:, :], WSv[s][:, k, (c % 4) * 128:(c % 4 + 1) * 128], hT[:, k, T * 512:(T + 1) * 512],
                       k == 0, k == 7, reads=[("WS", s)] + hres, writes=[psr(bk)])
                vcopy("act" if c % 2 == 0 else "dve", qb_[:, c, :], PS[bk][:, :], reads=[psr(bk)],
                      writes=[("Y", "qT", T % 2, c)])

        def xscores(T, h):
            qb_ = qTs[T % 2]
            pxs = []
            for mt in range(2):
                sb_ = scX.next()
                for cc in range(2):
                    mm(PS[sb_][:, :], KT[:, 2 * h + cc, mt * 128:(mt + 1) * 128], qb_[:, 2 * h + cc, :], cc == 0, cc == 1,
                       reads=[("Z", "KT", 2 * h + cc), ("Y", "qT", T % 2, 2 * h + cc)], writes=[psr(sb_)])
                pi = pxc[0] % 4
                pxc[0] += 1
                act(pX[pi][:, :], PS[sb_][:, :], AF.Exp, reads=[psr(sb_)], writes=[("Y", "pX", pi)], scale=1.0 / 16.0)
                pxs.append(pi)
            return (h, pxs)

        def xrest(item):
            h, pxs = item
            bd = mmX.next()
            for mt in range(2):
                mm(PS[bd][:, :], ONESB[:, :], pX[pxs[mt]][:, :], mt == 0, mt == 1,
                   reads=[("M", "ones"), ("Y", "pX", pxs[mt])], writes=[psr(bd)])
            recip(rdn, PS[bd][:, :], reads=[psr(bd)], writes=[("Z", "rdn")])
            for cc in range(2):
                bo = mmX.next()
                c = 2 * h + cc
                for mt in range(2):
                    mm(PS[bo][:, :], Vx[:, mt, c * 128:(c + 1) * 128], pX[pxs[mt]][:, :], mt == 0, mt == 1,
                       reads=[("Z", "Vx", mt, c // 4), ("Y", "pX", pxs[mt])], writes=[psr(bo)])
                tt("dve", oT[:, c, :], PS[bo][:, :], rdn, ALU.mult, reads=[psr(bo), ("Z", "rdn")],
                   writes=[("Y", "oT", c)])

        qproj(0)
        for T in range(4):
            prev = None
            for h in range(4):
                cur_h = xscores(T, h)
                if prev is not None:
                    xrest(prev)
                prev = cur_h
            if T + 1 < 4:
                qproj(T + 1)
            xrest(prev)
            for jj in range(4):
                t = 4 * T + jj
                for nb in range(2):
                    s = wo_[nb][1]
                    bk = mmX.next()
                    for c in range(8):
                        mm(PS[bk][:, :], oT[:, c, jj * 128:(jj + 1) * 128], WSv[s][:, c, :], c == 0, c == 7,
                           reads=[("WS", s), ("Y", "oT", c)], writes=[psr(bk)])
                    tt("dve", Xv[:, t, nb * 512:(nb + 1) * 512], Xv[:, t, nb * 512:(nb + 1) * 512], PS[bk][:, :], ALU.add,
                       reads=[("R0", "X", t, nb), psr(bk)], writes=[("R0", "X", t, nb)])
                cur = (norm_A(*xres(t), 0), t)
                if pendF is not None:
                    norm_B(pendF[0], hT, pendF[1], ("HT", pendF[1]))
                pendF = cur
        norm_B(pendF[0], hT, pendF[1], ("HT", pendF[1]))
        for it in wq_ + wo_:
            ws.release(it[0])
        load_grow(4, 0)

        P.fence("Y")
        P.fence("Z")
        p0gen = p0_iter(b + 1) if b + 1 < nseq else None

        def final_tile(t, b=b):
            src_, sres = xres(t)
            rs, rsr = rstd_stats(src_, sres, Z[:, 6144:7168], [("Z", "sg", 0)], 1024)
            stt(src_, src_, rs, GR[0][:], ALU.mult, ALU.mult, reads=list(sres) + rsr + [("M", "grow", 0)], writes=sres)
            o = dma("sp", out_d[b, t * 128:(t + 1) * 128, :], src_, ("outd", t % 2), reads=sres)
            out_dmas.append(o)

        def aT(f):
            if f < 16:
                return Y[:, f * 1024:(f + 1) * 1024], "Y"
            return Z[:, (f - 16) * 1024:(f - 15) * 1024], "Z"

        sg = [Zf[:, 3072 + i * 512:3072 + (i + 1) * 512] for i in range(2)]
        sgc = 0
        for half in range(2):
            for j in range(11):
                wi, s = ws.take("gu")
                for ff in range(2):
                    f = 2 * j + ff
                    av, aar = aT(f)
                    for blk in range(2):
                        T = half * 2 + blk
                        hres = [("HT", T * 4 + i) for i in range(4)]
                        bg = bigpool.next()
                        for k in range(8):
                            mm(PS[bg][:, :], WSv[s][:, k, ff * 256:ff * 256 + 128], hT[:, k, T * 512:(T + 1) * 512],
                               k == 0, k == 7, reads=[("WS", s)] + hres, writes=[psr(bg)])
                        si = sgc % 2
                        sgc += 1
                        act(sg[si], PS[bg][:, :], AF.Silu, reads=[psr(bg)], writes=[("Z", "sg", si)])
                        bu = bigpool.next()
                        for k in range(8):
                            mm(PS[bu][:, :], WSv[s][:, k, ff * 256 + 128:ff * 256 + 256], hT[:, k, T * 512:(T + 1) * 512],
                               k == 0, k == 7, reads=[("WS", s)] + hres, writes=[psr(bu)])
                        tt("dve", av[:, blk * 512:(blk + 1) * 512], sg[si], PS[bu][:, :], ALU.mult,
                           reads=[("Z", "sg", si), psr(bu)], writes=[(aar, "aT", f, blk)])
                ws.release(wi)
            for nb in range(2):
                wd = [ws.take("down") for _ in range(3)]
                for tl in range(8):
                    t = half * 8 + tl
                    bk = bigpool.next()
                    for f in range(22):
                        av, aar = aT(f)
                        s = wd[f // 8][1]
                        mm(PS[bk][:, :], av[:, tl * 128:(tl + 1) * 128], WSv[s][:, f % 8, :], f == 0, f == 21,
                           reads=[("WS", s), (aar, "aT", f, tl // 4)], writes=[psr(bk)])
                    tt("dve", Xv[:, t, nb * 512:(nb + 1) * 512], Xv[:, t, nb * 512:(nb + 1) * 512], PS[bk][:, :], ALU.add,
                       reads=[("R0", "X", t, nb), psr(bk)], writes=[("R0", "X", t, nb)])
                    if nb == 1:
                        final_tile(t)
                    if half == 1 and p0gen is not None:
                        next(p0gen)
                for it in wd:
                    ws.release(it[0])
        if p0gen is not None:
            for _ in p0gen:
                pass

    P.emit(nc, es, out_dmas)
    es.close()
    return nc


def _prep(inputs):
    f = lambda a: np.ascontiguousarray(np.asarray(a, dtype=np.float32))
    w_in = f(inputs["w_in"])[0]
    def rotcols(blk):
        n = blk.shape[1]
        idx = np.arange(n)
        idx = (idx // 64) * 64 + ((idx % 64) + 32) % 64
        return blk[:, idx]
    qa = w_in[:, 0:512]
    ka = w_in[:, 512:1024]
    va = w_in[:, 1024:1536]
    qb = w_in[:, 1536:2048]
    kb = w_in[:, 2048:2560]
    vb = w_in[:, 2560:3072]
    ga = w_in[:, 3072:4096]
    gb = w_in[:, 4096:5120]
    qar = rotcols(qa)
    kar = rotcols(ka)
    cols = []
    for h in range(4):
        sl = slice(h * 128, (h + 1) * 128)
        cols += [qa[:, sl], qar[:, sl], ka[:, sl], kar[:, sl]]
    cols.append(va)
    for m in range(4):
        sl = slice(m * 128, (m + 1) * 128)
        cols += [qb[:, sl], kb[:, sl]]
    cols.append(vb)
    for m in range(8):
        sl = slice(m * 128, (m + 1) * 128)
        cols += [ga[:, sl], gb[:, sl]]
    W1 = np.ascontiguousarray(np.concatenate(cols, axis=1))
    assert W1.shape == (1024, 6144)
    upa = f(inputs["w_up_a"])[0]
    upb = f(inputs["w_up_b"])[0]
    cols = []
    for m in range(8):
        sl = slice(m * 128, (m + 1) * 128)
        cols += [upa[:, sl], upb[:, sl]]
    Wup = np.ascontiguousarray(np.concatenate(cols, axis=1))
    wgu = f(inputs["w_gate_up"])[0]
    cols = []
    for j in range(22):
        cols += [wgu[:, j * 128:(j + 1) * 128], wgu[:, 2816 + j * 128:2816 + (j + 1) * 128]]
    Wgu = np.ascontiguousarray(np.concatenate(cols, axis=1))
    params = np.zeros((128, 272), np.float32)
    params[:, 0:64] = f(inputs["lam_q1"])[0][None, :]
    params[:, 64:128] = f(inputs["lam_k1"])[0][None, :]
    params[:, 128:192] = f(inputs["lam_q2"])[0][None, :]
    params[:, 192:256] = f(inputs["lam_k2"])[0][None, :]
    rel = f(inputs["rel_bias"])[0]
    params[:, 257:265] = rel[:, 512][None, :]
    params[64:128, 265] = -30000.0
    gains = np.stack([f(inputs["norm_mix_g"])[0], f(inputs["norm_cross_g"])[0], f(inputs["norm_mem_g"])[0],
                      f(inputs["norm_ffn_g"])[0], f(inputs["norm_final_g"])], axis=0)
    inv = (1.0 / (np.float32(10000.0) ** (np.arange(0, 64, 2, dtype=np.float32) / np.float32(64)))).astype(np.float32)
    ang = (np.arange(S, dtype=np.float32)[:, None] * inv[None, :]).astype(np.float32)
    cos = np.cos(ang).astype(np.float32).T
    sin = np.sin(ang).astype(np.float32).T
    p = np.arange(128)
    cosT = cos[p % 32]
    sgn = np.where((p % 64) < 32, -1.0, 1.0).astype(np.float32)[:, None]
    sinT = sin[p % 32] * sgn
    rope = np.ascontiguousarray(np.concatenate([cosT, sinT], axis=1)).astype(np.float32)
    kl = np.arange(128)[:, None]
    ql = np.arange(128)[None, :]
    biasB = np.zeros((128, 8, 4, 128), np.float32)
    for slot, t in enumerate((0, 1, 2, 4)):
        dist = 128 * t + ql - kl
        idx = np.clip(dist, -256, 256) + 256
        tile = rel[:, idx]
        if t == 0:
            invalid = (kl >= 64) & (ql < 64)
            tile = np.where(invalid[None], np.float32(-30000.0), tile)
        if t == 4:
            invalid = (kl < 64) & (ql >= 64)
            tile = np.where(invalid[None], np.float32(-30000.0), tile)
        biasB[:, :, slot, :] = np.transpose(tile, (1, 0, 2))
    biasB = np.ascontiguousarray(biasB.reshape(128, 4096))
    mm_ = np.arange(128)
    perm = np.zeros((128, 128), np.float32)
    perm[(mm_ // 64) * 64 + ((mm_ % 64) + 32) % 64, mm_] = 1.0
    common = dict(
        W1=W1, Wup=Wup, w_out=f(inputs["w_out"])[0], w_cq=f(inputs["w_cq"])[0], w_ckv=f(inputs["w_ckv"])[0],
        w_co=f(inputs["w_co"])[0], Wgu=Wgu, w_down=f(inputs["w_down"])[0], params=params, gains=gains,
        rope=rope, biasB=biasB, ident=np.eye(128, dtype=np.float32), sublng=f(inputs["subln_g"]), perm=perm,
    )
    return common


_NC_CACHE = {}


def kernel(**inputs):
    n = 8
    common = _prep(inputs)
    x = np.asarray(inputs["x"], dtype=np.float32)
    mem = np.asarray(inputs["mem"], dtype=np.float32)
    nseq = x.shape[0] // n
    if "nc" not in _NC_CACHE:
        _NC_CACHE["nc"] = build(nseq=nseq)
    nc = _NC_CACHE["nc"]
    in_maps = []
    for c in range(n):
        m = dict(common)
        m["x"] = np.ascontiguousarray(x[c * nseq:(c + 1) * nseq])
        m["mem"] = np.ascontiguousarray(mem[c * nseq:(c + 1) * nseq])
        in_maps.append(m)
    res = run_bass_kernel_spmd(nc, in_maps, core_ids=list(range(n)))
    return np.concatenate([np.asarray(r["out"], dtype=np.float32) for r in res.results], axis=0)
```

```python
import math
from contextlib import ExitStack

import numpy as np
import concourse.bass as bass
import concourse.mybir as mybir
from concourse.bass_utils import run_bass_kernel_spmd

F32 = mybir.dt.float32
BF16 = mybir.dt.bfloat16
AF = mybir.ActivationFunctionType
ALU = mybir.AluOpType

S = 2048
D = 1024
NT = 16
EPS = 1e-6
NSLOT = 5
LAM_INIT = 0.8 - 0.6 * math.exp(0.0)


class _Op:
    __slots__ = ("eng", "fn", "deps", "signal", "sig", "dma_key", "dma_val", "pos")


class Prog:
    ENGS = ("pe", "act", "dve", "pool", "sp")

    def __init__(self):
        self.streams = {e: [] for e in self.ENGS}
        self.res = {}
        self.dma_cnt = {}
        self.arena_last = {}
        self.fences = {}

    def fence(self, arena):
        cur = dict(self.fences.get(arena, {}))
        for k, o in self.arena_last.get(arena, {}).items():
            c = cur.get(k)
            if c is None or self._later(o, c):
                cur[k] = o
        self.fences[arena] = cur

    @staticmethod
    def _later(a, b):
        if a.dma_key is not None:
            return a.dma_val > b.dma_val
        return a.pos > b.pos

    def op(self, eng, fn, reads=(), writes=(), dma_key=None):
        o = _Op()
        o.eng = eng
        o.fn = fn
        o.signal = False
        o.sig = 0
        o.dma_key = dma_key
        o.dma_val = 0
        o.pos = len(self.streams[eng])
        if dma_key is not None:
            self.dma_cnt[dma_key] = self.dma_cnt.get(dma_key, 0) + 16
            o.dma_val = self.dma_cnt[dma_key]
        deps = {}

        def dep(p, raw):
            if p is None or p is o:
                return
            if p.dma_key is not None:
                k = ("d", p.dma_key)
                c = deps.get(k)
                if c is None or p.dma_val > c.dma_val:
                    deps[k] = p
                return
            if p.eng == eng and dma_key is None:
                if eng == "pe":
                    return
            c = deps.get(p.eng)
            if c is None or p.pos > c.pos:
                deps[p.eng] = p

        arenas = set()
        for r in reads:
            arenas.add(r[0])
            st = self.res.get(r)
            if st:
                dep(st[0], True)
        for w in writes:
            arenas.add(w[0])
            st = self.res.get(w)
            if st:
                dep(st[0], False)
                for rd in st[1].values():
                    dep(rd, False)
        for a in arenas:
            for p in self.fences.get(a, {}).values():
                dep(p, True)
        me = eng if dma_key is None else ("d", dma_key)
        for r in reads:
            st = self.res.setdefault(r, [None, {}])
            st[1][me] = o
        for w in writes:
            self.res[w] = [o, {}]
        for a in arenas:
            self.arena_last.setdefault(a, {})[me] = o
        o.deps = list(deps.values())
        for p in o.deps:
            if p.dma_key is None:
                p.signal = True
        self.streams[eng].append(o)
        return o

    def emit(self, nc, es, final_waits):
        esem = {e: es.enter_context(nc.semaphore("sem_" + e)) for e in self.ENGS}
        dsem = {}
        for i, k in enumerate(self.dma_cnt):
            dsem[k] = es.enter_context(nc.semaphore("dsem%d" % i))
        for e in self.ENGS:
            c = 0
            for o in self.streams[e]:
                if o.signal and o.dma_key is None:
                    c += 1
                    o.sig = c
        block = es.enter_context(nc.Block())

        def run(engname, eng):
            waited = {}

            def wait(p):
                if p.dma_key is not None:
                    k = ("d", p.dma_key)
                    s = dsem[p.dma_key]
                    v = p.dma_val
                else:
                    k = p.eng
                    s = esem[p.eng]
                    v = p.sig
                if waited.get(k, 0) >= v:
                    return
                waited[k] = v
                eng.wait_ge(s, v)

            for o in self.streams[engname]:
                for p in o.deps:
                    wait(p)
                inst = o.fn(eng)
                if o.dma_key is not None:
                    inst.then_inc(dsem[o.dma_key], 16)
                elif o.signal:
                    inst.then_inc(esem[engname], 1)
            if engname == "sp":
                last = {}
                for p in final_waits:
                    c = last.get(p.dma_key)
                    if c is None or p.dma_val > c.dma_val:
                        last[p.dma_key] = p
                for p in last.values():
                    wait(p)

        @block.tensor
        def _(e):
            run("pe", e)

        @block.scalar
        def _(e):
            run("act", e)

        @block.vector
        def _(e):
            run("dve", e)

        @block.gpsimd
        def _(e):
            run("pool", e)

        @block.sync
        def _(e):
            run("sp", e)


def build(nseq=2, stage=99, dump=None):
    nc = bass.Bass("TRN2", target_bir_lowering=False)
    P = Prog()
    es = ExitStack()

    def din(name, shape, dt=F32):
        return nc.dram_tensor(name, list(shape), dt, kind="ExternalInput").ap()

    x_d = din("x", [nseq, S, D])
    mem_d = din("mem", [nseq, 256, D])
    W1 = din("W1", [1024, 6144])
    Wup = din("Wup", [512, 2048])
    wout_d = din("w_out", [1024, 1024])
    wcq_d = din("w_cq", [1024, 1024])
    wckv_d = din("w_ckv", [1024, 2048])
    wco_d = din("w_co", [1024, 1024])
    Wgu = din("Wgu", [1024, 5632])
    wdown_d = din("w_down", [2816, 1024])
    params_d = din("params", [128, 272])
    gains_d = din("gains", [5, 1024])
    rope_d = din("rope", [128, 4096])
    biasB_d = din("biasB", [128, 4096])
    ident_d = din("ident", [128, 128])
    perm_d = din("perm", [128, 128])
    sublng_d = din("sublng", [1, 128])
    out_d = nc.dram_tensor("out", [nseq, S, D], F32, kind="ExternalOutput").ap()
    dump_d = None
    if dump is not None:
        dump_d = nc.dram_tensor("dump", list(dump[1]), dump[2], kind="ExternalOutput").ap()

    def sb(name, shape, dt):
        return es.enter_context(nc.sbuf_tensor(name, list(shape), dt))

    R0 = sb("R0", [128, 32768], BF16)
    HT = sb("HT", [128, 16384], BF16)
    Y = sb("Y", [128, 16384], BF16)
    Z = sb("Z", [128, 8192], BF16)
    WS = sb("WS", [128, NSLOT * 4096], BF16)
    IDB = sb("IDB", [128, 128], BF16)
    PERM = sb("PERM", [128, 128], BF16)
    ONESB = sb("ONESB", [128, 128], BF16)
    PRM = sb("PRM", [128, 272], F32)
    GROW = sb("GROW", [128, 1024], F32)
    GSROW = sb("GSROW", [128, 128], F32)
    XS = sb("XS", [128, 2048], F32)
    HN = sb("HN", [128, 2048], BF16)
    ST = sb("ST", [128, 64], F32)
    GROW2 = sb("GROW2", [128, 1024], F32)
    LTMP = sb("LTMP", [128, 128], F32)
    PS = [es.enter_context(nc.psum_tensor("ps%d" % i, [128, 512], F32)) for i in range(8)]

    R0f = R0[:].bitcast(F32)
    Zf = Z[:].bitcast(F32)
    Yf = Y[:].bitcast(F32)
    hT = HT[:].rearrange("p (k t) -> p k t", t=S)
    yaT = Y[:, 0:8192].rearrange("p (k t) -> p k t", t=S)
    ybT = Y[:, 8192:16384].rearrange("p (k t) -> p k t", t=S)
    WSv = [WS[:, s * 4096:(s + 1) * 4096].rearrange("p (k c) -> p k c", c=512) for s in range(NSLOT)]
    Xv = R0f.rearrange("p (t c) -> p t c", c=1024)
    cosT = Zf[:, 0:2048]
    sinT = Zf[:, 2048:4096]

    def r1(M):
        return M.rearrange("(k p) c -> p k c", p=128)

    W1r, Wupr, woutr, wcqr, wckvr, wcor, Wgur, wdownr = (r1(M) for M in (W1, Wup, wout_d, wcq_d, wckv_d, wco_d, Wgu, wdown_d))

    class Rot:
        def __init__(self, ids):
            self.ids = ids
            self.i = 0

        def next(self):
            b = self.ids[self.i % len(self.ids)]
            self.i += 1
            return b

    mmpool = Rot([0, 1, 2])
    scpool = Rot([4, 5])
    accpool = Rot([6, 7])

    def psr(b):
        return ("PS", b)

    loads = []

    def ld_block(Mr, c0, kind):
        def f(s):
            return [(WSv[s][:, :, :], Mr[:, :, c0:c0 + 512])]
        loads.append((kind, f))

    def ld_merge(m):
        def f(s):
            return [(WSv[s][:, :, 0:256], W1r[:, :, 4096 + m * 256:4096 + (m + 1) * 256]),
                    (WSv[s][:, 0:4, 256:512], Wupr[:, :, m * 256:(m + 1) * 256])]
        loads.append(("merge", f))

    def ld_down(nb, g):
        nk = min(8, 22 - 8 * g)

        def f(s):
            return [(WSv[s][:, 0:nk, :], wdownr[:, 8 * g:8 * g + nk, nb * 512:(nb + 1) * 512])]
        loads.append(("down", f))

    for _b in range(nseq):
        ld_block(W1r, 2048, "vA")
        for h in range(4):
            ld_block(W1r, h * 512, "A")
        if stage >= 3:
            ld_block(W1r, 3584, "vB")
            for i in range(2):
                ld_block(W1r, 2560 + i * 512, "Bqk")
        if stage >= 4:
            for m in range(8):
                ld_merge(m)
            for i in range(4):
                ld_block(wckvr, i * 512, "ckv")
            for nb in range(2):
                ld_block(woutr, nb * 512, "out")
        if stage >= 5:
            for i in range(2):
                ld_block(wcqr, i * 512, "cq")
            for i in range(2):
                ld_block(wcor, i * 512, "co")
        if stage >= 6:
            for half in range(2):
                for j in range(11):
                    ld_block(Wgur, j * 512, "gu")
                for nb in range(2):
                    for g in range(3):
                        ld_down(nb, g)

    class WStream:
        def __init__(self):
            self.emitted = 0
            self.released = 0
            self.taken = 0

        def pump(self, limit=NSLOT):
            while self.emitted < len(loads) and self.emitted < self.released + limit:
                i = self.emitted
                s = i % NSLOT
                for (dst, src) in loads[i][1](s):
                    P.op("pool", lambda e, dst=dst, src=src: e.dma_start(out=dst, in_=src),
                         writes=[("WS", s)], dma_key=("ws", s))
                self.emitted += 1

        def take(self, kind):
            i = self.taken
            assert loads[i][0] == kind, (loads[i][0], kind)
            assert i < self.emitted, "weight slot ring too small at load %d (%s)" % (i, kind)
            self.taken += 1
            return i, i % NSLOT

        def release(self, i):
            assert i == self.released, (i, self.released)
            self.released += 1
            self.pump()

    ws = WStream()

    def mm(out, lhsT, rhs, start, stop, reads, writes, skip=False):
        P.op("pe", lambda e: e.matmul(out, lhsT=lhsT, rhs=rhs, start=start, stop=stop, skip_group_check=skip),
             reads=reads, writes=writes)

    def tr(out, in_, reads, writes):
        P.op("pe", lambda e: e.transpose(out, in_, IDB[:]), reads=list(reads) + [("M", "idb")], writes=writes)

    def act(out, in_, func, reads, writes, scale=None, bias=None, accum=None):
        kw = {}
        if scale is not None:
            kw["scale"] = scale
        if bias is not None:
            kw["bias"] = bias
        if accum is not None:
            kw["accum_out"] = accum
        P.op("act", lambda e: e.activation(out=out, in_=in_, func=func, **kw), reads=reads, writes=writes)

    def vcopy(eng, out, in_, reads, writes):
        if eng == "act":
            P.op("act", lambda e: e.copy(out=out, in_=in_), reads=reads, writes=writes)
        else:
            P.op(eng, lambda e: e.tensor_copy(out=out, in_=in_), reads=reads, writes=writes)

    def tt(eng, out, in0, in1, op, reads, writes):
        P.op(eng, lambda e: e.tensor_tensor(out=out, in0=in0, in1=in1, op=op), reads=reads, writes=writes)

    def tsc(out, in0, s1, s2, op0, op1, reads, writes, eng="dve"):
        if op1 is None:
            P.op(eng, lambda e: e.tensor_scalar(out=out, in0=in0, scalar1=s1, scalar2=None, op0=op0),
                 reads=reads, writes=writes)
        else:
            P.op(eng, lambda e: e.tensor_scalar(out=out, in0=in0, scalar1=s1, scalar2=s2, op0=op0, op1=op1),
                 reads=reads, writes=writes)

    def stt(out, in0, scalar, in1, op0, op1, reads, writes):
        P.op("dve", lambda e: e.scalar_tensor_tensor(out=out, in0=in0, scalar=scalar, in1=in1, op0=op0, op1=op1),
             reads=reads, writes=writes)

    def recip(out, in_, reads, writes):
        P.op("dve", lambda e: e.reciprocal(out=out, in_=in_), reads=reads, writes=writes)

    def dma(eng, out, in_, key, reads=(), writes=()):
        return P.op(eng, lambda e: e.dma_start(out=out, in_=in_), reads=reads, writes=writes, dma_key=key)

    def memset(eng, ap, val, writes):
        P.op(eng, lambda e: e.memset(ap, val), writes=writes)

    stcol = [0]

    def newstat(n=1):
        c = stcol[0]
        if c + n > 64:
            c = 0
        stcol[0] = c + n
        return ST[:, c:c + n], [("M", "st", i) for i in range(c, c + n)]

    dma("sp", PRM[:], params_d, ("c", 0), writes=[("M", "prm")])
    dma("pool", IDB[:], ident_d, ("c", 1), writes=[("M", "idb")])
    dma("pool", PERM[:], perm_d, ("c", 3), writes=[("M", "perm")])
    dma("sp", GSROW[:], sublng_d[0:1, :].partition_broadcast(128), ("c", 2), writes=[("M", "gsrow")])
    memset("dve", ONESB[:], 1.0, writes=[("M", "ones")])
    tsc(GSROW[:], GSROW[:], 1.0 - LAM_INIT, None, ALU.mult, None, reads=[("M", "gsrow")], writes=[("M", "gsrow")])
    lt, ltr = newstat(4)
    tt("dve", LTMP[:, 0:64], PRM[:, 0:64], PRM[:, 64:128], ALU.mult, reads=[("M", "prm")], writes=[("M", "junkd")])
    P.op("dve", lambda e: e.tensor_reduce(out=lt[:, 0:1], in_=LTMP[:, 0:64], axis=mybir.AxisListType.X, op=ALU.add),
         reads=[("M", "junkd")], writes=[ltr[0]])
    tt("dve", LTMP[:, 64:128], PRM[:, 128:192], PRM[:, 192:256], ALU.mult, reads=[("M", "prm")], writes=[("M", "junkd2")])
    P.op("dve", lambda e: e.tensor_reduce(out=lt[:, 1:2], in_=LTMP[:, 64:128], axis=mybir.AxisListType.X, op=ALU.add),
         reads=[("M", "junkd2")], writes=[ltr[1]])
    act(lt[:, 2:4], lt[:, 0:2], AF.Exp, reads=ltr[0:2], writes=ltr[2:4])
    NEGLAM = sb("NEGLAM", [128, 2], F32)
    tt("dve", NEGLAM[:, 0:1], lt[:, 3:4], lt[:, 2:3], ALU.subtract, reads=ltr[2:4], writes=[("M", "nl0")])
    tsc(NEGLAM[:, 1:2], NEGLAM[:, 0:1], -LAM_INIT, None, ALU.add, None, reads=[("M", "nl0")], writes=[("M", "neglam")])
    neglam = NEGLAM[:, 1:2]
    ws.pump(limit=2)

    out_dmas = []

    GR = [GROW, GROW2]

    def load_grow(gi, slot):
        dma("sp", GR[slot][:], gains_d[gi:gi + 1, :].partition_broadcast(128), ("grow", slot),
            writes=[("M", "grow", slot)])

    ctr = {"hn": 0, "ev": 0, "xs": 0}

    def rstd_stats(src, sres, junk, junk_res, n_elems):
        ss, ssr = newstat()
        ln, lnr = newstat()
        rs, rsr = newstat()
        act(junk, src, AF.Square, reads=sres, writes=list(junk_res) + ssr, accum=ss)
        act(ln, ss, AF.Ln, reads=ssr, writes=lnr, scale=1.0 / n_elems, bias=EPS)
        act(rs, ln, AF.Exp, reads=lnr, writes=rsr, scale=-0.5)
        return rs, rsr

    def norm_A(src, sres, gslot):
        i = ctr["hn"] % 2
        ctr["hn"] += 1
        hn = HN[:, i * 1024:(i + 1) * 1024]
        hnr = ("M", "hn", i)
        rs, rsr = rstd_stats(src, sres, hn, [hnr], 1024)
        stt(hn, src, rs, GR[gslot][:], ALU.mult, ALU.mult, reads=list(sres) + rsr + [("M", "grow", gslot)],
            writes=[hnr])
        return hn, hnr

    def norm_B(pair, dstT, t, dres):
        hn, hnr = pair
        b_ = mmpool.next()
        pb = PS[b_][:].bitcast(BF16)
        for k in range(8):
            tr(pb[:, k * 128:(k + 1) * 128], hn[:, k * 128:(k + 1) * 128], reads=[hnr], writes=[psr(b_)])
        e_ = "dve" if ctr["ev"] % 2 == 0 else "act"
        ctr["ev"] += 1
        vcopy(e_, dstT[:, :, t * 128:(t + 1) * 128], pb.rearrange("p (k c) -> p k c", c=128), reads=[psr(b_)],
              writes=[dres])

    def xload(src_d, b, t):
        i = ctr["xs"] % 2
        ctr["xs"] += 1
        xs = XS[:, i * 1024:(i + 1) * 1024]
        dma("sp", xs, src_d[b, t * 128:(t + 1) * 128, :], ("xs", i), writes=[("M", "xs", i)])
        return xs, [("M", "xs", i)]

    def p0_iter(b, after_tile=None):
        pend = None
        nxt = xload(x_d, b, 0)
        for t in range(NT):
            xs, xsr = nxt
            if t + 1 < NT:
                nxt = xload(x_d, b, t + 1)
            cur = (norm_A(xs, xsr, 1), t)
            if pend is not None:
                norm_B(pend[0], hT, pend[1], ("HT", pend[1]))
                if after_tile is not None:
                    after_tile(pend[1])
            pend = cur
            yield
        norm_B(pend[0], hT, pend[1], ("HT", pend[1]))
        if after_tile is not None:
            after_tile(pend[1])
        yield

    def xres(t):
        return Xv[:, t, :], [("R0", "X", t, 0), ("R0", "X", t, 1)]

    def do_dump(ap, reads):
        o = dma("sp", dump_d, ap, ("dump",), reads=reads)
        out_dmas.append(o)

    assert stage == 99
    load_grow(0, 1)
    vAaug0 = R0[:, 12288:12288 + 16 * 4 * 129].rearrange("p (t h d) -> p t h d", h=4, d=129)
    mmV = Rot([1, 2, 3])

    def vA_tile(t, s, vAaug_):
        bk = mmV.next()
        for k in range(8):
            mm(PS[bk][:, :], hT[:, k, t * 128:(t + 1) * 128], WSv[s][:, k, :], k == 0, k == 7,
               reads=[("HT", t), ("WS", s)], writes=[psr(bk)])
        vcopy("act" if t % 2 == 0 else "dve", vAaug_[:, t, :, 0:128],
              PS[bk][:].rearrange("p (h d) -> p h d", d=128), reads=[psr(bk)], writes=[("R0", "vA", t)])

    wiv0, sv0 = ws.take("vA")
    for n_, _ in enumerate(p0_iter(0, after_tile=lambda t: vA_tile(t, sv0, vAaug0))):
        if n_ == 8:
            ws.pump()
    ws.release(wiv0)
    ws.pump()
    for b in range(nseq):
        for a in ("R0", "Y", "Z"):
            P.fence(a)
        dma("sp", Zf, rope_d, ("rope",), writes=[("Z", "rope")])

        OFF_QK = 0
        OFF_VA = 12288
        OFF_PT = 20608
        OFF_PTD = 22656
        OFF_ON0 = 26752
        OFF_OT = 27776
        OFF_YTOK = 28800
        qk = [R0[:, OFF_QK + i * 6144:OFF_QK + (i + 1) * 6144].rearrange("p (w t) -> p w t", t=S) for i in range(2)]
        vAaug = R0[:, OFF_VA:OFF_VA + 16 * 4 * 129].rearrange("p (t h d) -> p t h d", h=4, d=129)
        pTs = [R0[:, OFF_PT + i * 512:OFF_PT + (i + 1) * 512] for i in range(4)]
        pTd = [R0[:, OFF_PTD + i * 512:OFF_PTD + (i + 1) * 512] for i in range(8)]
        On0 = R0f[:, OFF_ON0 // 2:OFF_ON0 // 2 + 512].rearrange("p (j d) -> p j d", d=128)
        Ot = R0f[:, OFF_OT // 2:OFF_OT // 2 + 512].rearrange("p (j d) -> p j d", d=128)
        ytoks = [R0[:, OFF_YTOK + i * 512:OFF_YTOK + (i + 1) * 512].rearrange("p (j d) -> p j d", d=128) for i in range(2)]
        accsetsA = Rot([(6, 7), (2, 3)])
        mmA4 = Rot([0, 1, 2, 3])
        trA = Rot([0])
        scA = Rot([4, 5, 1])
        pend_tr = []
        ytog = [0]
        zb = [R0[:, 29824 + i * 512:29824 + (i + 1) * 512] for i in range(2)]
        zbc = [0]
        rt = [Yf[:, 4096 + i * 512:4096 + (i + 1) * 512] for i in range(4)]
        for i in range(2):
            memset("pool", qk[i][64:128, 1, :], 0.0, writes=[("R0", "kzz", i, 0)])
            memset("pool", qk[i][0:64, 2, :], 0.0, writes=[("R0", "kzz", i, 1)])

        memset("pool", vAaug[:, :, :, 128:129], 1.0, writes=[("R0", "vAones")])

        if b > 0:
            wi, s = ws.take("vA")
            for t in range(NT):
                vA_tile(t, s, vAaug)
            ws.release(wi)

        pti = [0]
        for h in range(4):
            wi, s = ws.take("A")
            qkb = qk[h % 2]
            pendR = None

            def rope_finish(item, h=h, qkb=qkb):
                T, w, bz, zi = item
                br = mmA4.next()
                mm(PS[br][:, :], PERM[:, :], zb[zi], True, True, reads=[("R0", "zb", zi), ("M", "perm")],
                   writes=[psr(br)])
                ia = 2 * (w % 2)
                tt("dve", rt[ia], PS[bz][:, :], cosT[:, T * 512:(T + 1) * 512], ALU.mult,
                   reads=[psr(bz), ("Z", "rope"), ("R0", "zb", zi)], writes=[("Y", "rt", ia)])
                tt("dve", rt[ia + 1], PS[br][:, :], sinT[:, T * 512:(T + 1) * 512], ALU.mult,
                   reads=[psr(br), ("Z", "rope")], writes=[("Y", "rt", ia + 1)])
                if w == 0:
                    tt("pool", qkb[:, 0, T * 512:(T + 1) * 512], rt[ia], rt[ia + 1], ALU.add,
                       reads=[("Y", "rt", ia), ("Y", "rt", ia + 1)], writes=[("R0", "qk", h % 2, 0, T)])
                else:
                    for c in range(2):
                        prc = slice(c * 64, c * 64 + 64)
                        tt("pool", qkb[prc, 1 + c, T * 512:(T + 1) * 512], rt[ia][prc, :], rt[ia + 1][prc, :], ALU.add,
                           reads=[("Y", "rt", ia), ("Y", "rt", ia + 1)], writes=[("R0", "qk", h % 2, 1 + c, T)])

            for T in range(4):
                for w in range(2):
                    bz = mmA4.next()
                    c0 = w * 256
                    for k in range(8):
                        mm(PS[bz][:, :], WSv[s][:, k, c0:c0 + 128], hT[:, k, T * 512:(T + 1) * 512], k == 0, k == 7,
                           reads=[("WS", s)] + [("HT", T * 4 + i) for i in range(4)], writes=[psr(bz)])
                    zi = zbc[0] % 2
                    zbc[0] += 1
                    vcopy("act", zb[zi], PS[bz][:, :], reads=[psr(bz)], writes=[("R0", "zb", zi)])
                    if pendR is not None:
                        rope_finish(pendR)
                    pendR = (T, w, bz, zi)
            rope_finish(pendR)
            ws.release(wi)
            if stage == 2 and dump is not None and dump[0] == "qk" and h == 0:
                do_dump(qkb[:, 0:2, :], [("R0", "qk", 0, w, T) for w in range(2) for T in range(4)])

            pendA = []

            def emit_pvA(item, h=h):
                g, i, buf, bres, r = item
                for jj in range(max(0, r), 4):
                    bkx = jj // 2
                    st_ = not g["started"][bkx]
                    g["started"][bkx] = True
                    accv = PS[g["accb"][bkx]][:, (jj % 2) * 256:(jj % 2) * 256 + 129]
                    mm(accv, buf[:, jj * 128:(jj + 1) * 128], vAaug[:, i, h, :], st_, i == g["last"],
                       reads=list(bres) + [("R0", "vA", i), ("R0", "vAones")], writes=[psr(g["accb"][bkx])], skip=True)
                if i == g["last"]:
                    finish_group(g)

            def finish_group(g, h=h):
                accb = g["accb"]
                Q = g["Q"]
                c = g["c"]

                def accv(jj):
                    return PS[accb[jj // 2]][:, (jj % 2) * 256:(jj % 2) * 256 + 129]

                rd, rdr = newstat(4)
                for bkx in range(2):
                    recip(rd[:, 2 * bkx:2 * bkx + 2], PS[accb[bkx]][:, 128:512:256], reads=[psr(accb[bkx])],
                          writes=rdr[2 * bkx:2 * bkx + 2])
                if c == 0:
                    for jj in range(4):
                        tsc(On0[:, jj, :], accv(jj)[:, 0:128], rd[:, jj:jj + 1], None, ALU.mult, None,
                            reads=[psr(accb[jj // 2]), rdr[jj]], writes=[("R0", "On0", jj)])
                    return
                rl, rlr = newstat(4)
                tsc(rl, rd, neglam, None, ALU.mult, None, reads=rdr + [("M", "neglam")], writes=rlr)
                ssq, ssr = newstat(4)
                yi = ytog[0] % 2
                ytog[0] += 1
                ytk = ytoks[yi]
                for jj in range(4):
                    stt(Ot[:, jj, :], accv(jj)[:, 0:128], rl[:, jj:jj + 1], On0[:, jj, :], ALU.mult, ALU.add,
                        reads=[psr(accb[jj // 2]), rlr[jj], ("R0", "On0", jj)], writes=[("R0", "Ot", jj)])
                    act(ytk[:, jj, :], Ot[:, jj, :], AF.Square, reads=[("R0", "Ot", jj)],
                        writes=[ssr[jj], ("R0", "ytok", yi, jj)], accum=ssq[:, jj:jj + 1])
                ln, lnr = newstat(4)
                rs, rsr = newstat(4)
                act(ln, ssq, AF.Ln, reads=ssr, writes=lnr, scale=1.0 / 128, bias=EPS)
                act(rs, ln, AF.Exp, reads=lnr, writes=rsr, scale=-0.5)
                for jj in range(4):
                    stt(ytk[:, jj, :], Ot[:, jj, :], rs[:, jj:jj + 1], GSROW[:], ALU.mult, ALU.mult,
                        reads=[("R0", "Ot", jj), rsr[jj], ("M", "gsrow")], writes=[("R0", "ytok", yi, jj)])

                def fin(h=h, Q=Q, ytk=ytk, yi=yi):
                    bt = trA.next()
                    pb = PS[bt][:].bitcast(BF16)
                    for jj in range(4):
                        tr(pb[:, jj * 128:(jj + 1) * 128], ytk[:, jj, :], reads=[("R0", "ytok", yi, jj)],
                           writes=[psr(bt)])
                    vcopy("dve", yaT[:, h, Q * 512:(Q + 1) * 512], pb[:, 0:512], reads=[psr(bt)],
                          writes=[("Y", "yaT", h, Q)])

                pend_tr.append([fin, 6])

            for Q in range(4):
                nkt = 4 * Q + 4
                for c in range(2):
                    diag = [4 * Q + 3, 4 * Q + 2, 4 * Q + 1, 4 * Q]
                    full = list(range(4 * Q))
                    order = []
                    while diag or full:
                        if diag:
                            order.append(diag.pop(0))
                        if full:
                            order.append(full.pop(0))
                    g = {"accb": list(accsetsA.next()), "started": [False, False], "nkt": nkt, "Q": Q, "c": c,
                         "last": order[-1]}
                    for i in order:
                        r = i - 4 * Q
                        c0 = max(0, r) * 128
                        sb_ = scA.next()
                        qres = [("R0", "qk", h % 2, 0, Q), ("R0", "qk", h % 2, 1 + c, i // 4), ("R0", "kzz", h % 2, c)]
                        mm(PS[sb_][:, c0:512], qkb[:, 1 + c, i * 128:(i + 1) * 128], qkb[:, 0, Q * 512 + c0:(Q + 1) * 512],
                           True, True, reads=qres, writes=[psr(sb_)])
                        if r < 0:
                            bi = pti[0] % 4
                            pti[0] += 1
                            buf = pTs[bi]
                            bres = [("R0", "pT", bi)]
                            act(buf[:, :], PS[sb_][:, :], AF.Exp, reads=[psr(sb_)], writes=bres, scale=0.125)
                        else:
                            bi = r + 4 * (pti[0] % 2)
                            pti[0] += 1
                            buf = pTd[bi]
                            bres = [("R0", "pTdd", bi), ("R0", "pTdd2", bi)]
                            act(buf[:, c0:c0 + 64], PS[sb_][:, c0:c0 + 64], AF.Exp, reads=[psr(sb_), ("M", "prm")],
                                writes=[bres[0]], scale=0.125, bias=PRM[:, 265:266])
                            act(buf[:, c0 + 64:512], PS[sb_][:, c0 + 64:512], AF.Exp, reads=[psr(sb_)],
                                writes=[bres[1]], scale=0.125)
                        pendA.append((g, i, buf, bres, r))
                        if len(pendA) > 2:
                            emit_pvA(pendA.pop(0))
                        for ent in list(pend_tr):
                            ent[1] -= 1
                            if ent[1] <= 0:
                                pend_tr.remove(ent)
                                ent[0]()
            while pendA:
                emit_pvA(pendA.pop(0))
        while pend_tr:
            pend_tr.pop(0)[0]()

        P.fence("R0")
        P.fence("Y")
        OFFB_V = 8192
        OFFB_BIAS = 16512
        OFFB_TMP = 24704
        OFFB_PT = 26752
        OFFB_YTOK = 29312
        qkB = [R0[:, i * 4096:(i + 1) * 4096].rearrange("p (w t) -> p w t", t=S) for i in range(2)]
        vBaug = R0[:, OFFB_V:OFFB_V + 16 * 8 * 65].rearrange("p (t h d) -> p t h d", h=8, d=65)
        biasT = R0f[:, OFFB_BIAS // 2:OFFB_BIAS // 2 + 4096].rearrange("p (h c) -> p h c", c=512)
        tmpS = [R0f[:, OFFB_TMP // 2 + i * 512:OFFB_TMP // 2 + (i + 1) * 512] for i in range(2)]
        pTB = [R0[:, OFFB_PT + i * 640:OFFB_PT + (i + 1) * 640] for i in range(4)]
        ybtok = R0[:, OFFB_YTOK:OFFB_YTOK + 2048].rearrange("p (j d) -> p j d", d=128)

        dma("sp", R0f[:, OFFB_BIAS // 2:OFFB_BIAS // 2 + 4096], biasB_d, ("biasB",), writes=[("R0", "biasT")])
        memset("pool", vBaug[:, :, :, 64:65], 1.0, writes=[("R0", "vBones")])

        mmB = Rot([0, 1])
        scpairs = Rot([(4, 2), (5, 3)])
        wiv, sv = ws.take("vB")
        for t in range(NT):
            bk = mmB.next()
            for k in range(8):
                mm(PS[bk][:, :], hT[:, k, t * 128:(t + 1) * 128], WSv[sv][:, k, :], k == 0, k == 7,
                   reads=[("HT", t), ("WS", sv)], writes=[psr(bk)])
            vcopy("act" if t % 2 == 0 else "dve", vBaug[:, t, :, 0:64],
                  PS[bk][:].rearrange("p (h d) -> p h d", d=64), reads=[psr(bk)], writes=[("R0", "vB", t)])
        ws.release(wiv)
        wq = [ws.take("Bqk"), ws.take("Bqk")]
        cnt = {"tmp": 0, "pt": 0}
        for m in range(4):
            wi, s = wq[m // 2]
            cb = (m % 2) * 256
            qkm = qkB[m % 2]
            for T in range(4):
                for w in range(2):
                    bk = mmB.next()
                    for k in range(8):
                        mm(PS[bk][:, :], WSv[s][:, k, cb + w * 128:cb + (w + 1) * 128], hT[:, k, T * 512:(T + 1) * 512],
                           k == 0, k == 7, reads=[("WS", s)] + [("HT", T * 4 + i) for i in range(4)], writes=[psr(bk)])
                    vcopy("act" if w == 0 else "dve", qkm[:, w, T * 512:(T + 1) * 512], PS[bk][:, :],
                          reads=[psr(bk)], writes=[("R0", "qkB", m % 2, w, T)])
            if m % 2 == 1:
                ws.release(wi)
            chains = []
            for hh in range(2):
                chains.append({"hh": hh, "h": 2 * m + hh, "pr": slice(hh * 64, hh * 64 + 64), "accb": 6 + hh,
                               "sc": [(4, 2), (5, 3)][hh], "pend": None, "k": 0})

            def emit_pvB(ch, item):
                j, tiles, ptb, ptres = item
                jj = j % 4
                accb = ch["accb"]
                h = ch["h"]
                accv = PS[accb][:, jj * 128:jj * 128 + 65]
                for n, (t, col) in enumerate(tiles):
                    mm(accv, ptb[:, col:col + 128], vBaug[:, j - t, h, :], n == 0, n == len(tiles) - 1,
                       reads=list(ptres) + [("R0", "vB", j - t), ("R0", "vBones")], writes=[psr(accb)], skip=True)

            def normB(ch, J):
                accb = ch["accb"]
                hh = ch["hh"]
                rd, rdr = newstat(4)
                recip(rd, PS[accb][:, 64:512:128], reads=[psr(accb)], writes=rdr)
                for jj in range(4):
                    j = 4 * J + jj
                    tsc(ybtok[:, j, hh * 64:(hh + 1) * 64], PS[accb][:, jj * 128:jj * 128 + 64], rd[:, jj:jj + 1], None,
                        ALU.mult, None, reads=[psr(accb), rdr[jj]], writes=[("R0", "ybtok", j, hh)])

            def stepB(ch, j):
                h = ch["h"]
                hh = ch["hh"]
                pr = ch["pr"]
                cbias = PRM[:, 257 + h:258 + h]
                sb_, sb3 = ch["sc"]
                tiles = []
                for t in range(min(3, j + 1)):
                    tiles.append((t, t * 128))
                nd = len(tiles)
                if j >= 4:
                    tiles.append((4, 384))
                    nd = 4
                for (t, col) in tiles:
                    mm(PS[sb_][:, col:col + 128], qkm[pr, 1, (j - t) * 128:(j - t + 1) * 128],
                       qkm[pr, 0, j * 128:(j + 1) * 128], True, True,
                       reads=[("R0", "qkB", m % 2, 1, (j - t) // 4), ("R0", "qkB", m % 2, 0, j // 4)],
                       writes=[psr(sb_)], skip=True)
                if j >= 3:
                    mm(PS[sb3][:, 0:128], qkm[pr, 1, (j - 3) * 128:(j - 2) * 128],
                       qkm[pr, 0, j * 128:(j + 1) * 128], True, True,
                       reads=[("R0", "qkB", m % 2, 1, (j - 3) // 4), ("R0", "qkB", m % 2, 0, j // 4)],
                       writes=[psr(sb3)], skip=True)
                ti = hh
                pi = 2 * hh + ch["k"] % 2
                ch["k"] += 1
                ptb = pTB[pi]
                ptres = [("R0", "pTB", pi, 0)]
                stt(tmpS[ti][:, 0:nd * 128], PS[sb_][:, 0:nd * 128], 0.125, biasT[:, h, 0:nd * 128], ALU.mult, ALU.add,
                    reads=[psr(sb_), ("R0", "biasT")], writes=[("R0", "tmpS", ti)])
                act(ptb[:, 0:nd * 128], tmpS[ti][:, 0:nd * 128], AF.Exp, reads=[("R0", "tmpS", ti)],
                    writes=[("R0", "pTB", pi, 0)])
                if j >= 3:
                    act(ptb[:, 512:640], PS[sb3][:, 0:128], AF.Exp, reads=[psr(sb3), ("M", "prm")],
                        writes=[("R0", "pTB", pi, 1)], scale=0.125, bias=cbias)
                    ptres.append(("R0", "pTB", pi, 1))
                    tiles = tiles + [(3, 512)]
                if ch["pend"] is not None:
                    pj = ch["pend"][0]
                    emit_pvB(ch, ch["pend"])
                    if pj % 4 == 3:
                        normB(ch, pj // 4)
                ch["pend"] = (j, tiles, ptb, ptres)

            for j in range(NT):
                for ch in chains:
                    stepB(ch, j)
            for ch in chains:
                emit_pvB(ch, ch["pend"])
                normB(ch, 3)
            for J in range(4):
                bt = mmB.next()
                pb = PS[bt][:].bitcast(BF16)
                for jj in range(4):
                    j = 4 * J + jj
                    tr(pb[:, jj * 128:(jj + 1) * 128], ybtok[:, j, :], reads=[("R0", "ybtok", j, 0), ("R0", "ybtok", j, 1)],
                       writes=[psr(bt)])
                vcopy("act", ybT[:, m, J * 512:(J + 1) * 512], pb[:, 0:512], reads=[psr(bt)], writes=[("Y", "ybT", m, J)])

        P.fence("R0")
        P.fence("Z")
        mnT = Z[:, 0:2048].rearrange("p (k t) -> p k t", t=256)
        KT = Z[:, 2048:4096].rearrange("p (k t) -> p k t", t=256)
        Vx = Z[:, 4096:6144].rearrange("p (mt c) -> p mt c", c=1024)
        rdn = Zf[:, 3072:3584]
        lnd = Zf[:, 3584:4096]
        load_grow(2, 1)
        for t in range(2):
            xs, xsr = xload(mem_d, b, t)
            norm_B(norm_A(xs, xsr, 1), mnT, t, ("Z", "mnT", t))
        if b + 1 < nseq:
            load_grow(0, 1)
        load_grow(1, 0)
        bigpool = Rot([0, 1, 2, 4, 5, 6, 7])
        mtmp = [R0f[:, i * 512:(i + 1) * 512] for i in range(8)]
        mT = R0[:, 16384:32768].rearrange("p (t m c) -> p t m c", m=8, c=128)
        mcount = 0
        for m in range(8):
            wi, s = ws.take("merge")
            for T in range(4):
                par = mcount % 2
                mcount += 1
                sa, sbb, t1, t2 = (mtmp[par * 4 + i] for i in range(4))
                rsa, rsb, rt1, rt2 = (("R0", "mtmp", par * 4 + i) for i in range(4))
                hres = [("HT", T * 4 + i) for i in range(4)]
                bA = bigpool.next()
                for k in range(8):
                    mm(PS[bA][:, :], WSv[s][:, k, 0:128], hT[:, k, T * 512:(T + 1) * 512], k == 0, k == 7,
                       reads=[("WS", s)] + hres, writes=[psr(bA)])
                act(sa, PS[bA][:, :], AF.Sigmoid, reads=[psr(bA)], writes=[rsa])
                bB = bigpool.next()
                for k in range(8):
                    mm(PS[bB][:, :], WSv[s][:, k, 128:256], hT[:, k, T * 512:(T + 1) * 512], k == 0, k == 7,
                       reads=[("WS", s)] + hres, writes=[psr(bB)])
                act(sbb, PS[bB][:, :], AF.Sigmoid, reads=[psr(bB)], writes=[rsb])
                bUa = bigpool.next()
                for k in range(4):
                    mm(PS[bUa][:, :], WSv[s][:, k, 256:384], yaT[:, k, T * 512:(T + 1) * 512], k == 0, k == 3,
                       reads=[("WS", s), ("Y", "yaT", k, T)], writes=[psr(bUa)])
                tt("dve", t1, sa, PS[bUa][:, :], ALU.mult, reads=[rsa, psr(bUa)], writes=[rt1])
                bUb = bigpool.next()
                for k in range(4):
                    mm(PS[bUb][:, :], WSv[s][:, k, 384:512], ybT[:, k, T * 512:(T + 1) * 512], k == 0, k == 3,
                       reads=[("WS", s), ("Y", "ybT", k, T)], writes=[psr(bUb)])
                tt("dve", t2, sbb, PS[bUb][:, :], ALU.mult, reads=[rsb, psr(bUb)], writes=[rt2])
                tt("pool", mT[:, 4 * T:4 * T + 4, m, :], t1.rearrange("p (j c) -> p j c", c=128),
                   t2.rearrange("p (j c) -> p j c", c=128), ALU.add, reads=[rt1, rt2], writes=[("R0", "mT", m, T)])
            ws.release(wi)

        mres = [("Z", "mnT", 0), ("Z", "mnT", 1)]
        for i in range(2):
            wi, s = ws.take("ckv")
            for cc in range(4):
                c = i * 4 + cc
                bk = mmpool.next()
                for k in range(8):
                    mm(PS[bk][:, 0:256], WSv[s][:, k, cc * 128:(cc + 1) * 128], mnT[:, k, :], k == 0, k == 7,
                       reads=[("WS", s)] + mres, writes=[psr(bk)])
                vcopy("act" if cc % 2 == 0 else "dve", KT[:, c, :], PS[bk][:, 0:256], reads=[psr(bk)],
                      writes=[("Z", "KT", c)])
            ws.release(wi)
        for nb in range(2):
            wi, s = ws.take("ckv")
            for mt in range(2):
                bk = mmpool.next()
                for k in range(8):
                    mm(PS[bk][:, :], mnT[:, k, mt * 128:(mt + 1) * 128], WSv[s][:, k, :], k == 0, k == 7,
                       reads=[("WS", s)] + mres, writes=[psr(bk)])
                vcopy("act" if mt == 0 else "dve", Vx[:, mt, nb * 512:(nb + 1) * 512], PS[bk][:, :], reads=[psr(bk)],
                      writes=[("Z", "Vx", mt, nb)])
            ws.release(wi)

        P.fence("R0")
        wo = [ws.take("out"), ws.take("out")]
        pend = None
        for t in range(NT):
            xs, xsr = xload(x_d, b, t)
            for nb in range(2):
                bk = mmpool.next()
                s = wo[nb][1]
                for m in range(8):
                    mm(PS[bk][:, :], mT[:, t, m, :], WSv[s][:, m, :], m == 0, m == 7,
                       reads=[("WS", s), ("R0", "mT", m, t // 4)], writes=[psr(bk)])
                tt("dve", Xv[:, t, nb * 512:(nb + 1) * 512], xs[:, nb * 512:(nb + 1) * 512], PS[bk][:, :], ALU.add,
                   reads=xsr + [psr(bk)], writes=[("R0", "X", t, nb)])
            cur = (norm_A(*xres(t), 0), t)
            if pend is not None:
                norm_B(pend[0], hT, pend[1], ("HT", pend[1]))
            pend = cur
        norm_B(pend[0], hT, pend[1], ("HT", pend[1]))
        ws.release(wo[0][0])
        ws.release(wo[1][0])
        load_grow(3, 0)

        P.fence("Y")
        mmX = Rot([0, 1, 2, 3, 7])
        qT = Y[:, 6144:10240].rearrange("p (k t) -> p k t", t=512)
        oT = Y[:, 10240:14336].rearrange("p (k t) -> p k t", t=512)
        pX = [Y[:, 14336 + i * 512:14336 + (i + 1) * 512] for i in range(4)]
        wq_ = [ws.take("cq"), ws.take("cq")]
        wo_ = [ws.take("co"), ws.take("co")]
        pxc = [0]
        pendF = None
        qTs = [qT, Y[:, 0:4096].rearrange("p (k t) -> p k t", t=512)]
        scX = Rot([4, 5, 6])

        def qproj(T):
            hres = [("HT", T * 4 + i) for i in range(4)]
            qb_ = qTs[T % 2]
            for c in range(8):
                s = wq_[c // 4][1]
                bk = mmX.next()
                for k in range(8):
                    mm(PS[bk][:, :], WSv[s][:, k, (c % 4) * 128:(c % 4 + 1) * 128], hT[:, k, T * 512:(T + 1) * 512],
                       k == 0, k == 7, reads=[("WS", s)] + hres, writes=[psr(bk)])
                vcopy("act" if c % 2 == 0 else "dve", qb_[:, c, :], PS[bk][:, :], reads=[psr(bk)],
                      writes=[("Y", "qT", T % 2, c)])

        def xscores(T, h):
            qb_ = qTs[T % 2]
            pxs = []
            for mt in range(2):
                sb_ = scX.next()
                for cc in range(2):
                    mm(PS[sb_][:, :], KT[:, 2 * h + cc, mt * 128:(mt + 1) * 128], qb_[:, 2 * h + cc, :], cc == 0, cc == 1,
                       reads=[("Z", "KT", 2 * h + cc), ("Y", "qT", T % 2, 2 * h + cc)], writes=[psr(sb_)])
                pi = pxc[0] % 4
                pxc[0] += 1
                act(pX[pi][:, :], PS[sb_][:, :], AF.Exp, reads=[psr(sb_)], writes=[("Y", "pX", pi)], scale=1.0 / 16.0)
                pxs.append(pi)
            return (h, pxs)

        def xrest(item):
            h, pxs = item
            bd = mmX.next()
            for mt in range(2):
                mm(PS[bd][:, :], ONESB[:, :], pX[pxs[mt]][:, :], mt == 0, mt == 1,
                   reads=[("M", "ones"), ("Y", "pX", pxs[mt])], writes=[psr(bd)])
            recip(rdn, PS[bd][:, :], reads=[psr(bd)], writes=[("Z", "rdn")])
            for cc in range(2):
                bo = mmX.next()
                c = 2 * h + cc
                for mt in range(2):
                    mm(PS[bo][:, :], Vx[:, mt, c * 128:(c + 1) * 128], pX[pxs[mt]][:, :], mt == 0, mt == 1,
                       reads=[("Z", "Vx", mt, c // 4), ("Y", "pX", pxs[mt])], writes=[psr(bo)])
                tt("dve", oT[:, c, :], PS[bo][:, :], rdn, ALU.mult, reads=[psr(bo), ("Z", "rdn")],
                   writes=[("Y", "oT", c)])

        qproj(0)
        for T in range(4):
            prev = None
            for h in range(4):
                cur_h = xscores(T, h)
                if prev is not None:
                    xrest(prev)
                prev = cur_h
            if T + 1 < 4:
                qproj(T + 1)
            xrest(prev)
            for jj in range(4):
                t = 4 * T + jj
                for nb in range(2):
                    s = wo_[nb][1]
                    bk = mmX.next()
                    for c in range(8):
                        mm(PS[bk][:, :], oT[:, c, jj * 128:(jj + 1) * 128], WSv[s][:, c, :], c == 0, c == 7,
                           reads=[("WS", s), ("Y", "oT", c)], writes=[psr(bk)])
                    tt("dve", Xv[:, t, nb * 512:(nb + 1) * 512], Xv[:, t, nb * 512:(nb + 1) * 512], PS[bk][:, :], ALU.add,
                       reads=[("R0", "X", t, nb), psr(bk)], writes=[("R0", "X", t, nb)])
                cur = (norm_A(*xres(t), 0), t)
                if pendF is not None:
                    norm_B(pendF[0], hT, pendF[1], ("HT", pendF[1]))
                pendF = cur
        norm_B(pendF[0], hT, pendF[1], ("HT", pendF[1]))
        for it in wq_ + wo_:
            ws.release(it[0])
        load_grow(4, 0)

        P.fence("Y")
        P.fence("Z")
        p0gen = p0_iter(b + 1) if b + 1 < nseq else None

        def final_tile(t, b=b):
            src_, sres = xres(t)
            rs, rsr = rstd_stats(src_, sres, Z[:, 6144:7168], [("Z", "sg", 0)], 1024)
            stt(src_, src_, rs, GR[0][:], ALU.mult, ALU.mult, reads=list(sres) + rsr + [("M", "grow", 0)], writes=sres)
            o = dma("sp", out_d[b, t * 128:(t + 1) * 128, :], src_, ("outd", t % 2), reads=sres)
            out_dmas.append(o)

        def aT(f):
            if f < 16:
                return Y[:, f * 1024:(f + 1) * 1024], "Y"
            return Z[:, (f - 16) * 1024:(f - 15) * 1024], "Z"

        sg = [Zf[:, 3072 + i * 512:3072 + (i + 1) * 512] for i in range(2)]
        sgc = 0
        for half in range(2):
            for j in range(11):
                wi, s = ws.take("gu")
                for ff in range(2):
                    f = 2 * j + ff
                    av, aar = aT(f)
                    for blk in range(2):
                        T = half * 2 + blk
                        hres = [("HT", T * 4 + i) for i in range(4)]
                        bg = bigpool.next()
                        for k in range(8):
                            mm(PS[bg][:, :], WSv[s][:, k, ff * 256:ff * 256 + 128], hT[:, k, T * 512:(T + 1) * 512],
                               k == 0, k == 7, reads=[("WS", s)] + hres, writes=[psr(bg)])
                        si = sgc % 2
                        sgc += 1
                        act(sg[si], PS[bg][:, :], AF.Silu, reads=[psr(bg)], writes=[("Z", "sg", si)])
                        bu = bigpool.next()
                        for k in range(8):
                            mm(PS[bu][:, :], WSv[s][:, k, ff * 256 + 128:ff * 256 + 256], hT[:, k, T * 512:(T + 1) * 512],
                               k == 0, k == 7, reads=[("WS", s)] + hres, writes=[psr(bu)])
                        tt("dve", av[:, blk * 512:(blk + 1) * 512], sg[si], PS[bu][:, :], ALU.mult,
                           reads=[("Z", "sg", si), psr(bu)], writes=[(aar, "aT", f, blk)])
                ws.release(wi)
            for nb in range(2):
                wd = [ws.take("down") for _ in range(3)]
                for tl in range(8):
                    t = half * 8 + tl
                    bk = bigpool.next()
                    for f in range(22):
                        av, aar = aT(f)
                        s = wd[f // 8][1]
                        mm(PS[bk][:, :], av[:, tl * 128:(tl + 1) * 128], WSv[s][:, f % 8, :], f == 0, f == 21,
                           reads=[("WS", s), (aar, "aT", f, tl // 4)], writes=[psr(bk)])
                    tt("dve", Xv[:, t, nb * 512:(nb + 1) * 512], Xv[:, t, nb * 512:(nb + 1) * 512], PS[bk][:, :], ALU.add,
                       reads=[("R0", "X", t, nb), psr(bk)], writes=[("R0", "X", t, nb)])
                    if nb == 1:
                        final_tile(t)
                    if half == 1 and p0gen is not None:
                        next(p0gen)
                for it in wd:
                    ws.release(it[0])
        if p0gen is not None:
            for _ in p0gen:
                pass

    P.emit(nc, es, out_dmas)
    es.close()
    return nc


def _prep(inputs):
    f = lambda a: np.ascontiguousarray(np.asarray(a, dtype=np.float32))
    w_in = f(inputs["w_in"])[0]
    def rotcols(blk):
        n = blk.shape[1]
        idx = np.arange(n)
        idx = (idx // 64) * 64 + ((idx % 64) + 32) % 64
        return blk[:, idx]
    qa = w_in[:, 0:512]
    ka = w_in[:, 512:1024]
    va = w_in[:, 1024:1536]
    qb = w_in[:, 1536:2048]
    kb = w_in[:, 2048:2560]
    vb = w_in[:, 2560:3072]
    ga = w_in[:, 3072:4096]
    gb = w_in[:, 4096:5120]
    qar = rotcols(qa)
    kar = rotcols(ka)
    cols = []
    for h in range(4):
        sl = slice(h * 128, (h + 1) * 128)
        cols += [qa[:, sl], qar[:, sl], ka[:, sl], kar[:, sl]]
    cols.append(va)
    for m in range(4):
        sl = slice(m * 128, (m + 1) * 128)
        cols += [qb[:, sl], kb[:, sl]]
    cols.append(vb)
    for m in range(8):
        sl = slice(m * 128, (m + 1) * 128)
        cols += [ga[:, sl], gb[:, sl]]
    W1 = np.ascontiguousarray(np.concatenate(cols, axis=1))
    assert W1.shape == (1024, 6144)
    upa = f(inputs["w_up_a"])[0]
    upb = f(inputs["w_up_b"])[0]
    cols = []
    for m in range(8):
        sl = slice(m * 128, (m + 1) * 128)
        cols += [upa[:, sl], upb[:, sl]]
    Wup = np.ascontiguousarray(np.concatenate(cols, axis=1))
    wgu = f(inputs["w_gate_up"])[0]
    cols = []
    for j in range(22):
        cols += [wgu[:, j * 128:(j + 1) * 128], wgu[:, 2816 + j * 128:2816 + (j + 1) * 128]]
    Wgu = np.ascontiguousarray(np.concatenate(cols, axis=1))
    params = np.zeros((128, 272), np.float32)
    params[:, 0:64] = f(inputs["lam_q1"])[0][None, :]
    params[:, 64:128] = f(inputs["lam_k1"])[0][None, :]
    params[:, 128:192] = f(inputs["lam_q2"])[0][None, :]
    params[:, 192:256] = f(inputs["lam_k2"])[0][None, :]
    rel = f(inputs["rel_bias"])[0]
    params[:, 257:265] = rel[:, 512][None, :]
    params[64:128, 265] = -30000.0
    gains = np.stack([f(inputs["norm_mix_g"])[0], f(inputs["norm_cross_g"])[0], f(inputs["norm_mem_g"])[0],
                      f(inputs["norm_ffn_g"])[0], f(inputs["norm_final_g"])], axis=0)
    inv = 1.0 / (10000.0 ** (np.arange(0, 64, 2, dtype=np.float64) / 64.0))
    ang = np.arange(S, dtype=np.float64)[:, None] * inv[None, :]
    cos = np.cos(ang).astype(np.float32).T
    sin = np.sin(ang).astype(np.float32).T
    p = np.arange(128)
    cosT = cos[p % 32]
    sgn = np.where((p % 64) < 32, -1.0, 1.0).astype(np.float32)[:, None]
    sinT = sin[p % 32] * sgn
    rope = np.ascontiguousarray(np.concatenate([cosT, sinT], axis=1)).astype(np.float32)
    kl = np.arange(128)[:, None]
    ql = np.arange(128)[None, :]
    biasB = np.zeros((128, 8, 4, 128), np.float32)
    for slot, t in enumerate((0, 1, 2, 4)):
        dist = 128 * t + ql - kl
        idx = np.clip(dist, -256, 256) + 256
        tile = rel[:, idx]
        if t == 0:
            invalid = (kl >= 64) & (ql < 64)
            tile = np.where(invalid[None], np.float32(-30000.0), tile)
        if t == 4:
            invalid = (kl < 64) & (ql >= 64)
            tile = np.where(invalid[None], np.float32(-30000.0), tile)
        biasB[:, :, slot, :] = np.transpose(tile, (1, 0, 2))
    biasB = np.ascontiguousarray(biasB.reshape(128, 4096))
    mm_ = np.arange(128)
    perm = np.zeros((128, 128), np.float32)
    perm[(mm_ // 64) * 64 + ((mm_ % 64) + 32) % 64, mm_] = 1.0
    common = dict(
        W1=W1, Wup=Wup, w_out=f(inputs["w_out"])[0], w_cq=f(inputs["w_cq"])[0], w_ckv=f(inputs["w_ckv"])[0],
        w_co=f(inputs["w_co"])[0], Wgu=Wgu, w_down=f(inputs["w_down"])[0], params=params, gains=gains,
        rope=rope, biasB=biasB, ident=np.eye(128, dtype=np.float32), sublng=f(inputs["subln_g"]), perm=perm,
    )
    return common


_NC_CACHE = {}


def kernel(**inputs):
    n = 8
    common = _prep(inputs)
    x = np.asarray(inputs["x"], dtype=np.float32)
    mem = np.asarray(inputs["mem"], dtype=np.float32)
    nseq = x.shape[0] // n
    if "nc" not in _NC_CACHE:
        _NC_CACHE["nc"] = build(nseq=nseq)
    nc = _NC_CACHE["nc"]
    in_maps = []
    for c in range(n):
        m = dict(common)
        m["x"] = np.ascontiguousarray(x[c * nseq:(c + 1) * nseq])
        m["mem"] = np.ascontiguousarray(mem[c * nseq:(c + 1) * nseq])
        in_maps.append(m)
    res = run_bass_kernel_spmd(nc, in_maps, core_ids=list(range(n)))
    return np.concatenate([np.asarray(r["out"], dtype=np.float32) for r in res.results], axis=0)
```
